# Optimizing a Trainium2 kernel written in Bass

```python
import math
import jax, jax.numpy as jnp
from jax import lax
import numpy as np

D_MODEL = 1024
BATCH = 16
SEQ = 2048
DEPTH = 1

MIX_WIDTH = D_MODEL
ATTN_WIDTH = D_MODEL // 2
SGU_WIDTH = MIX_WIDTH - ATTN_WIDTH
HEAD_DIM = 64
N_HEADS = ATTN_WIDTH // HEAD_DIM
N_KV = 2
GQ = N_HEADS // N_KV
KV_WIDTH = N_KV * HEAD_DIM
WINDOW = 128
BLK = 128
SGU_GROUPS = 8
SGU_GROUP_DIM = SGU_WIDTH // SGU_GROUPS
CHUNK = 128
D_FF = ((8 * D_MODEL // 3) + 127) // 128 * 128
CONV_WIDTH = 3
PLE_DIM = 256
IN_WIDTH = ATTN_WIDTH + 2 * KV_WIDTH + 2 * SGU_WIDTH
RMS_EPS = 1e-6
LN_EPS = 1e-5
NEG_INF = -1e30

kernel_name = "hymba_style_swa_sgu_convffn_ple_encoder"


def rmsnorm(x, g):
    xf = x.astype(jnp.float32)
    y = xf * lax.rsqrt(jnp.mean(xf * xf, axis=-1, keepdims=True) + RMS_EPS)
    return (y * g.astype(jnp.float32)).astype(x.dtype)


def layernorm(x, g, b):
    xf = x.astype(jnp.float32)
    mu = jnp.mean(xf, axis=-1, keepdims=True)
    var = jnp.mean(jnp.square(xf - mu), axis=-1, keepdims=True)
    y = (xf - mu) * lax.rsqrt(var + LN_EPS)
    return (y * g.astype(jnp.float32) + b.astype(jnp.float32)).astype(x.dtype)


def alibi_slopes(n_heads):
    return jnp.exp2(-8.0 * jnp.arange(1, n_heads + 1, dtype=jnp.float32) / n_heads)


def band_blocks(t, nb):
    b = t.shape[0]
    tp = jnp.pad(t, ((0, 0), (BLK, BLK), (0, 0), (0, 0)))
    tp = tp.reshape(b, nb + 2, BLK, t.shape[2], t.shape[3])
    return jnp.concatenate([tp[:, :-2], tp[:, 1:-1], tp[:, 2:]], axis=2)


def windowed_gqa(q, k, v, sink):
    b, s = q.shape[0], q.shape[1]
    nb = s // BLK
    qb = q.reshape(b, nb, BLK, N_KV, GQ, HEAD_DIM)
    kb = band_blocks(k, nb)
    vb = band_blocks(v, nb)
    scores = jnp.einsum('bnqkgd,bnskd->bnkgqs', qb, kb).astype(jnp.float32)
    scores = scores * (HEAD_DIM ** -0.5)
    qi = jnp.arange(BLK)[:, None]
    kj = jnp.arange(3 * BLK)[None, :]
    dist = jnp.abs(qi + BLK - kj)
    key_pos = jnp.arange(nb)[:, None] * BLK - BLK + jnp.arange(3 * BLK)[None, :]
    valid = (dist <= WINDOW)[None] & ((key_pos >= 0) & (key_pos < s))[:, None, :]
    slopes = alibi_slopes(N_HEADS).reshape(N_KV, GQ)
    bias = -slopes[:, :, None, None] * dist.astype(jnp.float32)
    scores = jnp.where(valid[None, :, None, None], scores + bias[None, None], NEG_INF)
    sink_l = sink.astype(jnp.float32).reshape(N_KV, GQ)[None, None, :, :, None, None]
    m = jnp.maximum(jnp.max(scores, axis=-1, keepdims=True), sink_l)
    e = jnp.exp(scores - m)
    denom = jnp.sum(e, axis=-1, keepdims=True) + jnp.exp(sink_l - m)
    probs = (e / denom).astype(v.dtype)
    o = jnp.einsum('bnkgqs,bnskd->bnqkgd', probs, vb)
    return o.reshape(b, s, N_HEADS * HEAD_DIM)


def chunked_sgu(zu, zv, ln_g, ln_b, w_s, b_s):
    b, s = zu.shape[0], zu.shape[1]
    nc = s // CHUNK
    u = jax.nn.gelu(zu, approximate=True)
    vv = layernorm(jax.nn.gelu(zv, approximate=True), ln_g, ln_b)
    vv = vv.reshape(b, nc, CHUNK, SGU_GROUPS, SGU_GROUP_DIM)
    mixed = jnp.einsum('hts,bnshc->bnthc', w_s, vv) + b_s.T[None, None, :, :, None]
    return u * mixed.reshape(b, s, SGU_WIDTH)


def dwconv3_centred(h, w, bias):
    hp = jnp.pad(h, ((0, 0), (1, 1), (0, 0)))
    return hp[:, :-2] * w[0] + hp[:, 1:-1] * w[1] + hp[:, 2:] * w[2] + bias


def setup_inputs(seed: int = 0) -> dict:
    key = jax.random.key(seed)
    ks = jax.random.split(key, 24)
    f32 = jnp.float32
    nrm = lambda k, shape, scale: jax.random.normal(k, shape, f32) * scale
    gain = lambda k, shape: 1.0 + 0.02 * jax.random.normal(k, shape, f32)
    return {
        "x": jax.random.normal(ks[0], (BATCH, SEQ, D_MODEL), f32),
        "p": jax.random.normal(ks[1], (DEPTH, BATCH, SEQ, PLE_DIM), f32),
        "g_mix": gain(ks[2], (DEPTH, D_MODEL)),
        "w_in": nrm(ks[3], (DEPTH, D_MODEL, IN_WIDTH), D_MODEL ** -0.5),
        "attn_sink": nrm(ks[4], (DEPTH, N_HEADS), 0.5),
        "sgu_ln_g": gain(ks[5], (DEPTH, SGU_WIDTH)),
        "sgu_ln_b": nrm(ks[6], (DEPTH, SGU_WIDTH), 0.02),
        "sgu_w": nrm(ks[7], (DEPTH, SGU_GROUPS, CHUNK, CHUNK), CHUNK ** -0.5),
        "sgu_b": 1.0 + nrm(ks[8], (DEPTH, SGU_GROUPS, CHUNK), 0.02),
        "g_attn_out": gain(ks[9], (DEPTH, ATTN_WIDTH)),
        "g_sgu_out": gain(ks[10], (DEPTH, SGU_WIDTH)),
        "w_out": nrm(ks[11], (DEPTH, MIX_WIDTH, D_MODEL), MIX_WIDTH ** -0.5),
        "g_ffn": gain(ks[12], (DEPTH, D_MODEL)),
        "w_up": nrm(ks[13], (DEPTH, D_MODEL, 2 * D_FF), D_MODEL ** -0.5),
        "conv_w": nrm(ks[14], (DEPTH, CONV_WIDTH, D_FF), CONV_WIDTH ** -0.5),
        "conv_b": nrm(ks[15], (DEPTH, D_FF), 0.02),
        "w_down": nrm(ks[16], (DEPTH, D_FF, D_MODEL), D_FF ** -0.5),
        "g_ple": gain(ks[17], (DEPTH, D_MODEL)),
        "w_ple_gate": nrm(ks[18], (DEPTH, D_MODEL, D_MODEL), D_MODEL ** -0.5),
        "w_ple_proj": nrm(ks[19], (DEPTH, PLE_DIM, D_MODEL), PLE_DIM ** -0.5),
        "g_final": gain(ks[20], (D_MODEL,)),
    }


def reference(x, p, g_mix, w_in, attn_sink, sgu_ln_g, sgu_ln_b, sgu_w, sgu_b,
              g_attn_out, g_sgu_out, w_out, g_ffn, w_up, conv_w, conv_b, w_down,
              g_ple, w_ple_gate, w_ple_proj, g_final):
    b, s, _ = x.shape
    h = x
    splits = [ATTN_WIDTH, ATTN_WIDTH + KV_WIDTH, ATTN_WIDTH + 2 * KV_WIDTH,
              ATTN_WIDTH + 2 * KV_WIDTH + SGU_WIDTH]
    for i in range(DEPTH):
        a = rmsnorm(h, g_mix[i])
        z = a @ w_in[i]
        q, k, v, zu, zv = jnp.split(z, splits, axis=-1)
        q = q.reshape(b, s, N_HEADS, HEAD_DIM)
        k = k.reshape(b, s, N_KV, HEAD_DIM)
        v = v.reshape(b, s, N_KV, HEAD_DIM)
        attn_o = windowed_gqa(q, k, v, attn_sink[i])
        sgu_o = chunked_sgu(zu, zv, sgu_ln_g[i], sgu_ln_b[i], sgu_w[i], sgu_b[i])
        merged = jnp.concatenate([rmsnorm(attn_o, g_attn_out[i]),
                                  rmsnorm(sgu_o, g_sgu_out[i])], axis=-1)
        h = h + merged @ w_out[i]
        c = rmsnorm(h, g_ffn[i])
        gate, val = jnp.split(c @ w_up[i], 2, axis=-1)
        gate = dwconv3_centred(gate, conv_w[i], conv_b[i])
        h = h + (jax.nn.gelu(gate, approximate=True) * val) @ w_down[i]
        pg = jax.nn.sigmoid(rmsnorm(h, g_ple[i]) @ w_ple_gate[i])
        h = h + (p[i] @ w_ple_proj[i]) * pg
    return rmsnorm(h, g_final)
```

```python
import numpy as np
import concourse.bass as bass
import concourse.mybir as mybir
from concourse.bass_utils import run_bass_kernel_spmd

dt = mybir.dt
AF = mybir.ActivationFunctionType
ALU = mybir.AluOpType
F32, BF16 = dt.float32, dt.bfloat16
ESZ = {dt.float32: 4, dt.bfloat16: 2, dt.uint8: 1, dt.int32: 4, dt.uint32: 4,
       dt.float16: 2, dt.uint16: 2, dt.int16: 2}

GRAN = 64
SB_BYTES = 207 * 1024
NLANE = 8


class Op:
    __slots__ = ("src", "idx", "issuer", "emit", "waits", "signals", "clock", "inc")


class Sched:
    def __init__(self, nc):
        self.nc = nc
        self.engines = ["pe", "act", "dve", "pool", "sp"]
        self.sources = list(self.engines)
        self.lanes = {}
        for q in ("sp", "pool", "act"):
            self.lanes[q] = [f"dma_{q}{i}" for i in range(NLANE)]
            self.sources += self.lanes[q]
        self.sidx = {s: i for i, s in enumerate(self.sources)}
        ns = len(self.sources)
        self.nsrc = ns
        self.src_ops = {s: [] for s in self.sources}
        self.stream = {e: [] for e in self.engines}
        self.know = {e: np.full(ns, -1, np.int64) for e in self.engines}
        ng = SB_BYTES // GRAN
        self.sbW = np.full((ng, ns), -1, np.int64)
        self.sbR = np.full((ng, ns), -1, np.int64)
        self.psW = np.full((8, ns), -1, np.int64)
        self.psR = np.full((8, ns), -1, np.int64)
        self.lane_rr = {q: 0 for q in self.lanes}
        self.big = nc.alloc_sbuf_tensor("big", [128, SB_BYTES], dt.uint8)
        self.psum = nc.alloc_psum_tensor("ps", [128, 8, 512], dt.float32)
        self.ps_rr = 0
        self.n_waits = 0

    def sb_view(self, off, shape, dtype):
        n = int(np.prod(shape))
        nb = n * ESZ[dtype]
        assert off % ESZ[dtype] == 0 and off + nb <= SB_BYTES, (off, nb)
        v = self.big[:, off:off + nb].bitcast(dtype)
        if len(shape) == 2:
            v = v.rearrange("p (a b) -> p a b", a=shape[0])
        elif len(shape) == 3:
            v = v.rearrange("p (a b c) -> p a b c", a=shape[0], b=shape[1])
        return v

    def ps_next(self):
        b = self.ps_rr % 8
        self.ps_rr += 1
        return b

    def ps_bank(self, b, dtype=dt.float32):
        v = self.psum[:, b, :]
        if dtype != dt.float32:
            v = v.bitcast(dtype)
        return v

    @staticmethod
    def _range(ap):
        t = ap.tensor
        esz = ESZ[ap.dtype]
        per_part = int(np.prod(list(t.shape)[1:]))
        col0 = ap.offset % per_part
        ext = 1
        for (st, cnt) in list(ap.ap)[1:]:
            ext += (cnt - 1) * abs(st)
        is_ps = "PSum" in type(t).__name__
        return is_ps, col0 * esz, (col0 + ext) * esz

    def _granules(self, ap):
        is_ps, lo, hi = self._range(ap)
        if is_ps:
            return True, lo // 2048, (hi - 1) // 2048 + 1
        return False, lo // GRAN, (hi - 1) // GRAN + 1

    def _record(self, src, issuer, emit, reads, writes, inc):
        op = Op()
        op.src, op.issuer, op.emit, op.inc = src, issuer, emit, inc
        op.idx = len(self.src_ops[src])
        op.signals = False
        need = np.full(self.nsrc, -1, np.int64)
        rg = [self._granules(ap) for ap in reads]
        wg = [self._granules(ap) for ap in writes]
        wg = wg + [g for g in rg if g[0]]
        rg = [g for g in rg if not g[0]]
        for (ps, a, b) in rg:
            np.maximum(need, self.sbW[a:b].max(axis=0), out=need)
        for (ps, a, b) in wg:
            W, R = (self.psW, self.psR) if ps else (self.sbW, self.sbR)
            np.maximum(need, W[a:b].max(axis=0), out=need)
            np.maximum(need, R[a:b].max(axis=0), out=need)
        si = self.sidx[src]
        if src != issuer and op.idx > 0:
            need[si] = max(need[si], op.idx - 1)
        if src == "pe":
            need[si] = -1
        know = self.know[issuer]
        waits = []
        for s in np.nonzero(need > know)[0]:
            sname = self.sources[s]
            dep = self.src_ops[sname][need[s]]
            dep.signals = True
            waits.append((sname, int(need[s])))
            np.maximum(know, dep.clock, out=know)
        op.waits = waits
        self.n_waits += len(waits)
        clk = know.copy()
        clk[si] = max(clk[si], op.idx)
        op.clock = clk
        for (ps, a, b) in rg:
            self.sbR[a:b, si] = op.idx
        for (ps, a, b) in wg:
            W, R = (self.psW, self.psR) if ps else (self.sbW, self.sbR)
            W[a:b, :] = -1
            W[a:b, si] = op.idx
            R[a:b, :] = -1
        self.src_ops[src].append(op)
        self.stream[issuer].append(op)
        return op

    def op(self, eng, emit, reads=(), writes=()):
        return self._record(eng, eng, emit, list(reads), list(writes), 1)

    def dma(self, q, out, in_, after=(), **kw):
        lane = self.lanes[q][self.lane_rr[q] % NLANE]
        self.lane_rr[q] += 1
        reads = ([in_] if "DRam" not in type(in_.tensor).__name__ else []) + list(after)
        writes = [out] if "DRam" not in type(out.tensor).__name__ else []
        op = self._record(lane, q, lambda e: e.dma_start(out=out, in_=in_, **kw), reads, writes, 16)
        op.signals = True
        return op

    def emit_all(self, final_wait_ops=()):
        nc = self.nc
        cnt = {}
        for s in self.sources:
            c = 0
            arr = []
            for o in self.src_ops[s]:
                if o.signals:
                    c += o.inc
                arr.append(c)
            cnt[s] = arr
        used = [s for s in self.sources if self.src_ops[s]]
        from contextlib import ExitStack
        with ExitStack() as st:
            sems = {s: st.enter_context(nc.semaphore(f"s_{s}")) for s in used}
            block = st.enter_context(nc.Block())

            def run(ename):
                def body(e):
                    for o in self.stream[ename]:
                        for (s, i) in o.waits:
                            e.wait_ge(sems[s], cnt[s][i])
                        ins = o.emit(e)
                        if o.signals:
                            ins.then_inc(sems[o.src], o.inc)
                    if ename == "sp":
                        for o in final_wait_ops:
                            e.wait_ge(sems[o.src], cnt[o.src][o.idx])
                return body

            block.tensor(run("pe"))
            block.scalar(run("act"))
            block.vector(run("dve"))
            block.gpsimd(run("pool"))
            block.sync(run("sp"))


class Ring:
    def __init__(self, views):
        self.v = views
        self.i = 0

    def next(self):
        r = self.v[self.i % len(self.v)]
        self.i += 1
        return r


D = 1024
SEQ = 2048
NB = 16
NSEQ = 2
DFF = 2816
NFC = 22
WIN_COLS = 1920
G_FFN = 4
PASSES = [(0, 4), (4, 4), (8, 4), (12, 4), (16, 3), (19, 3)]
WINDOWS = [(0, 3), (3, 3), (6, 3), (9, 3), (12, 3), (15, 1)]
RMS_EPS = 1e-6
LN_EPS = 1e-5
WBANK = 24576


def build_nc(debug=False):
    nc = bass.Bass("TRN2", target_bir_lowering=False)

    def din(name, shape):
        return nc.dram_tensor(name, list(shape), F32, kind="ExternalInput").ap()

    x_d = din("x", [NSEQ, SEQ, D])
    p_d = din("p", [NSEQ, SEQ, 256])
    win_d = din("w_in_r", [D, WIN_COLS])
    wout_d = din("w_out", [D, D])
    wup_d = din("w_up", [D, 2 * DFF])
    wdn_d = din("w_down", [DFF, D])
    wpg_d = din("w_pg", [D, D])
    wpp_d = din("w_pp", [256, D])
    gvec_d = din("gvec", [128, 32])
    convp_d = din("convp", [128, NFC * 4])
    bs8_d = din("bs8", [8, 128])
    ind_d = din("ind16", [16, 512])
    wsT_d = din("wsT", [128, 8 * 128])
    dtab_d = din("dtab", [128, 3 * 128])
    ident_d = din("ident", [128, 128])
    sink_d = din("sink", [8])
    lng_d = din("lng", [512])
    lnb_d = din("lnb", [512])
    gfin_d = din("gfin", [D])
    out_d = nc.dram_tensor("out", [NSEQ, SEQ, D], F32, kind="ExternalOutput").ap()
    if debug:
        dbg1 = nc.dram_tensor("dbg1", [NSEQ, SEQ, D], F32, kind="ExternalOutput").ap()
        dbg2 = nc.dram_tensor("dbg2", [NSEQ, SEQ, D], F32, kind="ExternalOutput").ap()

    S = Sched(nc)
    top = [0]

    def I(eng, fname, *args, r=(), w=(), **kw):
        return S.op(eng, lambda e: getattr(e, fname)(*args, **kw), reads=r, writes=w)

    def alloc(shape, dtype, align=64):
        n = int(np.prod(shape)) * ESZ[dtype]
        off = (top[0] + align - 1) // align * align
        top[0] = off + n
        assert top[0] <= SB_BYTES, f"SBUF overflow {top[0]}"
        return S.sb_view(off, shape, dtype)

    h = alloc([NB, D], F32)
    W_OFF = top[0]
    top[0] += 3 * WBANK
    ident = alloc([128], BF16)
    gvec = alloc([32], F32)
    convp = alloc([NFC, 4], F32)
    expb = alloc([3, 2, 512], BF16)
    bs2x = alloc([128], BF16)
    ind16 = alloc([512], BF16)
    sinkb = alloc([8], F32)
    expsink = alloc([8], F32)
    lng_bc = alloc([512], F32)
    lnb_bc = alloc([512], F32)
    wsT = alloc([8, 128], BF16)
    junk = alloc([1024], BF16)
    rs_mix = alloc([16], F32)
    rs_ffn = alloc([16], F32)
    rs_ple = alloc([16], F32)
    stat_ring = Ring([alloc([16], F32) for _ in range(46)])
    A_OFF = top[0]

    win = S.sb_view(W_OFF, [8, WIN_COLS], BF16)
    wout = S.sb_view(W_OFF + 8 * WIN_COLS * 2, [8, D], BF16)

    def ffn_bank_views(bank, G):
        o = W_OFF + bank * WBANK
        upg = S.sb_view(o, [8, G * 128], BF16)
        upv = S.sb_view(o + 8 * G * 128 * 2, [8, G * 128], BF16)
        dn = S.sb_view(o + 16 * G * 128 * 2, [G, D], BF16)
        return upg, upv, dn

    def p3_views(bank):
        o = W_OFF + bank * WBANK
        return S.sb_view(o, [8, D], BF16), S.sb_view(o + 8 * D * 2, [2, D], BF16)

    for b in range(6):
        S.dma("sp", h[:, b, :], x_d[0, b * 128:(b + 1) * 128, :])
    S.dma("pool", ident[:, :], ident_d)
    S.dma("sp", gvec[:, :], gvec_d)
    S.dma("sp", convp[:, :, :], convp_d.rearrange("p (c f) -> p c f", f=4))
    S.dma("sp", sinkb[:, :], sink_d.partition_broadcast(128))
    S.dma("sp", lng_bc[:, :], lng_d.partition_broadcast(128))
    S.dma("sp", lnb_bc[:, :], lnb_d.partition_broadcast(128))
    S.dma("pool", wsT[:, :, :], wsT_d.rearrange("p (h t) -> p h t", h=8))
    I("act", "activation", expsink[:, :], sinkb[:, :], AF.Exp, r=[sinkb], w=[expsink])
    g_mix, g_ffn, g_ple, g_mrg = gvec[:, 0:8], gvec[:, 8:16], gvec[:, 16:24], gvec[:, 24:32]
    top[0] = A_OFF
    dtab = alloc([3, 128], F32)
    bsf = alloc([128], F32)
    bsd = alloc([128], F32)
    bslo = alloc([128], BF16)
    S.dma("sp", dtab[:, :, :], dtab_d.rearrange("p (j q) -> p j q", j=3))
    S.dma("sp", bsf[0:8, :], bs8_d)
    S.dma("sp", bsf[8:16, :], bs8_d)
    S.dma("pool", ind16[0:16, :], ind_d)
    for j in range(3):
        for g in range(2):
            for par in range(2):
                for hh in range(2):
                    hd = 4 * g + 2 * hh + par
                    c0 = par * 256 + hh * 128
                    I("act", "activation", expb[:, j, g, c0:c0 + 128], dtab[:, j, :], AF.Exp, scale=-(2.0 ** (-(hd + 1))),
                      r=[dtab[:, j, :]], w=[expb[:, j, g, c0:c0 + 128]])
    I("dve", "tensor_copy", bs2x[0:16, :], bsf[0:16, :], r=[bsf], w=[bs2x])
    I("dve", "tensor_tensor", bsd[0:16, :], bsf[0:16, :], bs2x[0:16, :], ALU.subtract, r=[bsf, bs2x], w=[bsd])
    I("dve", "tensor_copy", bslo[0:16, :], bsd[0:16, :], r=[bsd], w=[bslo])
    S.dma("sp", bs2x[8:16, :], bslo[8:16, :])

    def scale_rows(wv, gcol):
        I("dve", "tensor_scalar", wv, wv, gcol, None, ALU.mult, r=[wv, gcol], w=[wv])

    def wdma(dst, src, after=()):
        S.dma("pool", dst, src, after=after)

    def load_p1_weights_a():
        for kk in range(0, 6, 2):
            wdma(win[:, kk:kk + 2, :], win_d[kk * 128:(kk + 2) * 128, :].rearrange("(k p) n -> p k n", p=128))
        return [lambda k=k: scale_rows(win[:, k, :], g_mix[:, k:k + 1]) for k in range(6)]

    def load_p1_weights_b():
        wdma(win[:, 6:8, :], win_d[6 * 128:8 * 128, :].rearrange("(k p) n -> p k n", p=128))
        for kk in range(0, 8, 4):
            wdma(wout[:, kk:kk + 4, :], wout_d[kk * 128:(kk + 4) * 128, :].rearrange("(k p) n -> p k n", p=128))
        return ([lambda k=k: scale_rows(win[:, k, :], g_mix[:, k:k + 1]) for k in (6, 7)]
                + [lambda k=k: scale_rows(wout[:, k, :], g_mrg[:, k:k + 1]) for k in range(8)])

    def load_p1_weights_startup():
        groups = []
        for (c0, c1) in ((896, 1920), (0, 896)):
            for kk in range(0, 8, 4):
                wdma(win[:, kk:kk + 4, c0:c1], win_d[kk * 128:(kk + 4) * 128, c0:c1].rearrange("(k p) n -> p k n", p=128))
        groups.append([lambda k=k: scale_rows(win[:, k, 896:1920], g_mix[:, k:k + 1]) for k in range(8)])
        groups.append([lambda k=k: scale_rows(win[:, k, 0:896], g_mix[:, k:k + 1]) for k in range(8)])
        groups.append([lambda k=k: scale_rows(wout[:, k, :], g_mrg[:, k:k + 1]) for k in range(8)])
        return groups

    def load_wout_startup():
        for kk in range(0, 8, 4):
            wdma(wout[:, kk:kk + 4, :], wout_d[kk * 128:(kk + 4) * 128, :].rearrange("(k p) n -> p k n", p=128),
                 after=[win[:, 7, 0:64]])

    def load_ffn_pass(pi, bank, after=()):
        c0, G = PASSES[pi]
        upg, upv, dn = ffn_bank_views(bank, G)
        wdma(upg[:, :, :], wup_d[:, c0 * 128:(c0 + G) * 128].rearrange("(k p) n -> p k n", p=128), after=after)
        wdma(upv[:, :, :], wup_d[:, DFF + c0 * 128:DFF + (c0 + G) * 128].rearrange("(k p) n -> p k n", p=128), after=after)
        wdma(dn[:, :, :], wdn_d[c0 * 128:(c0 + G) * 128, :].rearrange("(c p) n -> p c n", p=128), after=after)
        return ([lambda k=k: scale_rows(upg[:, k, :], g_ffn[:, k:k + 1]) for k in range(8)]
                + [lambda k=k: scale_rows(upv[:, k, :], g_ffn[:, k:k + 1]) for k in range(8)])

    def load_p3_weights(bank):
        wg, wp = p3_views(bank)
        for kk in range(0, 8, 4):
            wdma(wg[:, kk:kk + 4, :], wpg_d[kk * 128:(kk + 4) * 128, :].rearrange("(k p) n -> p k n", p=128))
        wdma(wp[:, :, :], wpp_d.rearrange("(k p) n -> p k n", p=128))
        return [lambda k=k: scale_rows(wg[:, k, :], g_ple[:, k:k + 1]) for k in range(8)]

    def run(thunks):
        for t in thunks:
            t()

    def interleave(*gens):
        gens = [g for g in gens if g is not None]
        while gens:
            for g in list(gens):
                try:
                    next(g)
                except StopIteration:
                    gens.remove(g)

    def chain(*gens):
        for g in gens:
            if g is not None:
                yield from g

    def rstd_from(src, n, eps, reads, dst=None):
        st = stat_ring.next()
        ss, lnv, rs = st[:, 0:1], st[:, 1:2], st[:, 2:3]
        if dst is not None:
            rs = dst
        I("act", "activation", junk[:, 0:n], src, AF.Square, accum_out=ss, r=reads, w=[ss, junk[:, 0:n]])
        I("act", "activation", lnv, ss, AF.Ln, scale=1.0 / n, bias=eps, r=[ss], w=[lnv])
        I("act", "activation", rs, lnv, AF.Exp, scale=-0.5, r=[lnv], w=[rs])
        return rs

    def transposes(src, nchunks):
        b = S.ps_next()
        pb = S.ps_bank(b, BF16)
        for c in range(nchunks):
            I("pe", "transpose", pb[:, c * 128:(c + 1) * 128], src[:, c * 128:(c + 1) * 128], ident[:, :],
                 r=[src[:, c * 128:(c + 1) * 128], ident], w=[pb[:, c * 128:(c + 1) * 128]])
        return pb[:, 0:nchunks * 128].rearrange("p (c t) -> p c t", c=nchunks)

    def norm_transpose(hb, xn, dst, rs=None, evac="act"):
        if rs is None:
            rs = rstd_from(hb, D, RMS_EPS, [hb])
        I("dve", "tensor_scalar", xn, hb, rs, None, ALU.mult, r=[hb, rs], w=[xn])
        pT = transposes(xn, 8)
        if evac == "act":
            I("act", "activation", dst, pT, AF.Copy, r=[pT], w=[dst])
        else:
            I("dve", "tensor_copy", dst, pT, r=[pT], w=[dst])

    out_stores = []

    pending_p1_scales = []
    for seq in range(NSEQ):
        top[0] = A_OFF
        kz = [alloc([2, 768], BF16) for _ in range(2)]
        vaug = alloc([6, 2, 65], BF16)
        qT = alloc([4, 768], BF16)
        NSG = 5
        sgT = alloc([4, NSG * 128], BF16)
        XG_OFF = (top[0] + 63) // 64 * 64
        xg_bytes = [alloc([512], F32) for _ in range(2)]
        xg_r = Ring([(v.bitcast(BF16), v) for v in xg_bytes])
        aT = alloc([8, 256], BF16)
        u_r = Ring([alloc([512], BF16) for _ in range(2)])
        t2_r = Ring([alloc([512], BF16) for _ in range(2)])
        n_r = Ring([alloc([512], BF16) for _ in range(2)])
        sgo_r = Ring([alloc([512], BF16) for _ in range(2)])
        e_r = Ring([alloc([3, 512], BF16) for _ in range(4)])
        ao_r = Ring([alloc([512], BF16) for _ in range(2)])
        aoT_r = Ring([alloc([4, 128], BF16) for _ in range(2)])
        rs_ring = [None] * NSG
        p1_top = top[0]

        def tslot(b):
            return ((b // 2) % 3) * 256 + (b % 2) * 128

        def vslot(b):
            return ((b // 2) % 3) * 2 + (b % 2)

        if seq == 0:
            w_groups = load_p1_weights_startup()
            run(w_groups[0])
        else:
            run(pending_p1_scales)
            pending_p1_scales = []

        I("dve", "memset", kz[0][64:128, :, :], 0.0, w=[kz[0]])
        I("dve", "memset", kz[1][0:64, :, :], 0.0, w=[kz[1]])
        I("dve", "memset", vaug[:, :, :, 64:65], 1.0, w=[vaug])

        def load_x(b):
            S.dma("sp", h[:, b, :], x_d[seq, b * 128:(b + 1) * 128, :])

        def stageA_block(b):
            bb = b % 2
            hb = h[:, b, :]
            xn, gv = xg_r.next()
            I("dve", "tensor_scalar", xn, hb, rs_mix[:, b:b + 1], None, ALU.mult, r=[hb, rs_mix[:, b:b + 1]], w=[xn])
            yield
            pTx = transposes(xn, 8)
            I("dve", "tensor_copy", aT[:, :, bb * 128:(bb + 1) * 128], pTx, r=[pTx], w=[aT[:, :, bb * 128:(bb + 1) * 128]])
            yield
            lhs = lambda k: aT[:, k, bb * 128:(bb + 1) * 128]
            bv, bu, bz = S.ps_next(), S.ps_next(), S.ps_next()
            pv_, pu_, pz_ = S.ps_bank(bv), S.ps_bank(bu), S.ps_bank(bz)
            for (pb, c0, c1) in ((pz_[:, :], 1408, 1920), (pu_[:, :], 896, 1408), (pv_[:, 0:128], 768, 896)):
                for k in range(8):
                    I("pe", "matmul", pb, lhs(k), win[:, k, c0:c1], start=(k == 0), stop=(k == 7),
                      r=[lhs(k), win[:, k, c0:c1]], w=[pb])
            u = u_r.next()
            I("act", "activation", gv[:, :], pz_[:, :], AF.Gelu_apprx_tanh, r=[pz_], w=[gv])
            I("act", "activation", u[:, :], pu_[:, :], AF.Gelu_apprx_tanh, r=[pu_], w=[u])
            vs = vslot(b)
            I("act", "activation", vaug[:, vs, :, 0:64], pv_[:, 0:128].rearrange("p (g d) -> p g d", g=2), AF.Copy,
              r=[pv_[:, 0:128]], w=[vaug[:, vs, :, 0:64]])
            yield
            st = stat_ring.next()
            st2 = stat_ring.next()
            bst, mv = st[:, 0:6], st[:, 8:10]
            lnv, rln = st2[:, 0:1], st2[:, 1:2]
            I("dve", "bn_stats", bst, gv[:, :], r=[gv], w=[bst])
            I("dve", "bn_aggr", mv, bst, r=[bst], w=[mv])
            I("act", "activation", lnv, mv[:, 1:2], AF.Ln, bias=LN_EPS, r=[mv], w=[lnv])
            I("act", "activation", rln, lnv, AF.Exp, scale=-0.5, r=[lnv], w=[rln])
            t2 = t2_r.next()
            nn = n_r.next()
            I("dve", "scalar_tensor_tensor", t2[:, :], gv[:, :], mv[:, 0:1], lng_bc[:, :], ALU.subtract, ALU.mult,
              r=[gv, mv, lng_bc], w=[t2])
            I("dve", "scalar_tensor_tensor", nn[:, :], t2[:, :], rln, lnb_bc[:, :], ALU.mult, ALU.add,
              r=[t2, rln, lnb_bc], w=[nn])
            yield
            bs_ = S.ps_next()
            psb = S.ps_bank(bs_)
            for hh in range(8):
                cs = slice(hh * 64, (hh + 1) * 64)
                I("pe", "matmul", psb[:, cs], wsT[:, hh, :], nn[:, cs], start=(hh == 0), stop=False, skip_group_check=True,
                  r=[wsT[:, hh, :], nn[:, cs]], w=[psb[:, cs]])
            I("pe", "matmul", psb[:, :], bs2x[0:16, :], ind16[0:16, :], start=False, stop=True, skip_group_check=True,
              r=[bs2x, ind16], w=[psb])
            sgo = sgo_r.next()
            I("dve", "tensor_tensor", sgo[:, :], psb[:, :], u[:, :], ALU.mult, r=[psb, u], w=[sgo])
            rs_b = rstd_from(sgo[:, :], 512, RMS_EPS, [sgo])
            I("dve", "tensor_scalar", sgo[:, :], sgo[:, :], rs_b, None, ALU.mult, r=[sgo, rs_b], w=[sgo])
            yield
            pT = transposes(sgo, 4)
            slot = b % NSG
            I("dve", "tensor_copy", sgT[:, :, slot * 128:(slot + 1) * 128], pT,
              r=[pT], w=[sgT[:, :, slot * 128:(slot + 1) * 128]])
            yield

        def stageA_qk(st_i):
            so = (st_i % 3) * 256
            for i in range(3):
                b_ = S.ps_next()
                pb = S.ps_bank(b_)
                for cc in range(2):
                    c = 2 * i + cc
                    for k in range(8):
                        I("pe", "matmul", pb[:, cc * 256:(cc + 1) * 256], win[:, k, c * 128:(c + 1) * 128], aT[:, k, :], start=(k == 0), stop=(k == 7),
                          r=[win[:, k, c * 128:(c + 1) * 128], aT[:, k, :]], w=[pb[:, cc * 256:(cc + 1) * 256]])
                src3 = pb[:, 0:512].rearrange("p (c t) -> p c t", c=2)
                if i < 2:
                    I("dve", "tensor_copy", qT[:, 2 * i:2 * i + 2, so:so + 256], src3,
                      r=[pb[:, 0:512]], w=[qT[:, 2 * i:2 * i + 2, so:so + 256]])
                else:
                    for par in range(2):
                        ps_ = slice(par * 64, (par + 1) * 64)
                        I("act", "activation", kz[par][ps_, :, so:so + 256], src3[ps_, :, :], AF.Copy,
                          r=[pb[:, 0:512]], w=[kz[par][:, :, so:so + 256]])
                yield

        def stageB(n):
            hb = h[:, n, :]
            qo = tslot(n)
            js = [j for j in range(3) if 0 <= n - 1 + j < NB]
            ao = ao_r.next()
            st = stat_ring.next()
            den, rden = st[:, 0:8], st[:, 8:16]
            Es = [e_r.next(), e_r.next()]
            for g in range(2):
                E = Es[g]
                for j in js:
                    kb = n - 1 + j
                    ko = tslot(kb)
                    b_ = S.ps_next()
                    pb = S.ps_bank(b_)
                    for par in range(2):
                        I("pe", "matmul", pb[:, par * 256:(par + 1) * 256], kz[par][:, g, ko:ko + 128],
                          qT[:, 2 * g:2 * g + 2, qo:qo + 128], start=True, stop=True,
                          r=[kz[par][:, g, ko:ko + 128], qT[:, 2 * g:2 * g + 2, qo:qo + 128]],
                          w=[pb[:, par * 256:(par + 1) * 256]])
                    I("act", "activation", E[:, j, :], pb[:, :], AF.Exp, scale=0.125, r=[pb], w=[E[:, j, :]])
                    I("dve", "tensor_tensor", E[:, j, :], E[:, j, :], expb[:, j, g, :], ALU.mult,
                      r=[E[:, j, :], expb[:, j, g, :]], w=[E[:, j, :]])
            yield
            for g in range(2):
                E = Es[g]
                bp = S.ps_next()
                pvb = S.ps_bank(bp)
                for idx in range(4):
                    hh, par = idx // 2, idx % 2
                    c0 = par * 256 + hh * 128
                    for j in js:
                        kb = n - 1 + j
                        I("pe", "matmul", pvb[:, idx * 65:(idx + 1) * 65], E[:, j, c0:c0 + 128], vaug[:, vslot(kb), g, :],
                          start=(j == js[0]), stop=(j == js[-1]),
                          r=[E[:, j, c0:c0 + 128], vaug[:, vslot(kb), g, :]], w=[pvb[:, idx * 65:(idx + 1) * 65]])
                pv3 = pvb[:, 0:260].rearrange("p (h d) -> p h d", h=4)
                dg, rg_ = den[:, 4 * g:4 * g + 4], rden[:, 4 * g:4 * g + 4]
                I("dve", "tensor_tensor", dg, pv3[:, :, 64], expsink[:, 4 * g:4 * g + 4], ALU.add,
                  r=[pvb[:, 0:260], expsink], w=[dg])
                I("dve", "reciprocal", rg_, dg, r=[dg], w=[rg_])
                ao3 = ao[:, g * 256:(g + 1) * 256].rearrange("p (h d) -> p h d", h=4)
                I("dve", "tensor_tensor", ao3, pv3[:, :, 0:64], rg_.unsqueeze(2).to_broadcast([128, 4, 64]), ALU.mult,
                  r=[pvb[:, 0:260], rg_], w=[ao[:, g * 256:(g + 1) * 256]])
            yield
            r_a = rstd_from(ao[:, :], 512, RMS_EPS, [ao])
            I("dve", "tensor_scalar", ao[:, :], ao[:, :], r_a, None, ALU.mult, r=[ao, r_a], w=[ao])
            yield
            pT = transposes(ao, 4)
            aoT = aoT_r.next()
            I("dve", "tensor_copy", aoT[:, :, :], pT, r=[pT], w=[aoT])
            yield
            slot = n % NSG
            for half in range(2):
                hs = slice(half * 512, (half + 1) * 512)
                ba = S.ps_next()
                pa = S.ps_bank(ba)
                for c in range(4):
                    I("pe", "matmul", pa[:, :], sgT[:, c, slot * 128:(slot + 1) * 128], wout[:, 4 + c, hs], start=(c == 0), stop=False,
                      r=[sgT[:, c, slot * 128:(slot + 1) * 128], wout[:, 4 + c, hs]], w=[pa])
                for c in range(4):
                    I("pe", "matmul", pa[:, :], aoT[:, c, :], wout[:, c, hs], start=False, stop=(c == 3),
                      r=[aoT[:, c, :], wout[:, c, hs]], w=[pa])
                I("dve", "tensor_tensor", hb[:, hs], pa[:, :], hb[:, hs], ALU.add,
                  r=[pa, hb[:, hs]], w=[hb[:, hs]])
                yield
            rstd_from(hb, D, RMS_EPS, [hb], dst=rs_ffn[:, n:n + 1])
            yield

        def delayed(gen, n):
            for _ in range(n):
                yield
            yield from gen

        def streamsA(st_i):
            if st_i >= 8:
                return [None, None, None]
            return [stageA_block(2 * st_i), stageA_block(2 * st_i + 1), delayed(stageA_qk(st_i), 2)]

        def streamR(st_i):
            if st_i >= 8:
                return
            for bq in (2 * st_i, 2 * st_i + 1):
                rstd_from(h[:, bq, :], D, RMS_EPS, [h[:, bq, :]], dst=rs_mix[:, bq:bq + 1])
                yield

        def streamsB(st_i):
            gs = [stageB(n) if n >= 0 else None for n in (2 * st_i - 1, 2 * st_i)]
            return gs

        if seq > 0:
            for b in range(6):
                load_x(b)
        interleave(streamR(0), streamR(1))
        gA0 = streamsA(0)
        if seq == 0:
            for g_ in gA0:
                next(g_)
            run(w_groups[1])
            load_wout_startup()
            ffn_sc = load_ffn_pass(0, 2, after=[win[:, 7, 0:64]])
            run(w_groups[2])
        interleave(*gA0)
        if seq > 0:
            ffn_sc = load_ffn_pass(0, 2, after=[aT[:, 0, 0:8]])
        for st_i in range(8):
            for b in (2 * st_i + 6, 2 * st_i + 7):
                if b < NB:
                    load_x(b)
            extra = []
            if st_i == 7:
                e_r.v += [S.sb_view(XG_OFF, [3, 512], BF16), S.sb_view(XG_OFF + 3072, [3, 512], BF16)]
                ao_r.v += [u_r.v[0]]
                aoT_r.v += [u_r.v[1].rearrange("p (c t) -> p c t", c=4)]
                extra = [stageB(NB - 1)]
            interleave(*(streamsA(st_i + 1) + streamsB(st_i) + extra + [streamR(st_i + 2)]))
            if st_i == 1:
                run(ffn_sc)
        if debug:
            for b in range(NB):
                S.dma("sp", dbg1[seq, b * 128:(b + 1) * 128, :], h[:, b, :])

        top[0] = A_OFF
        cT = alloc([8, SEQ + 2], BF16)
        uT_r = Ring([alloc([G_FFN, 384], BF16) for _ in range(3)])
        t1_r = Ring([alloc([384], F32) for _ in range(2)])
        ge_r = Ring([alloc([384], BF16) for _ in range(2)])
        xn2_r = Ring([alloc([D], BF16) for _ in range(2)])
        I("dve", "memset", cT[:, :, 0:1], 0.0, w=[cT[:, :, 0:1]])
        I("dve", "memset", cT[:, :, SEQ + 1:SEQ + 2], 0.0, w=[cT[:, :, SEQ + 1:SEQ + 2]])

        def make_cT(w):
            if w >= len(WINDOWS):
                return
            b0, nb = WINDOWS[w]
            for b in range(b0, b0 + nb):
                norm_transpose(h[:, b, :], xn2_r.next(), cT[:, :, 1 + b * 128:1 + (b + 1) * 128], rs=rs_ffn[:, b:b + 1])
                yield

        def ffn_window(pi, bank, w):
            c0, G = PASSES[pi]
            upg, upv, dn = ffn_bank_views(bank, G)
            b0, nb = WINDOWS[w]
            T = nb * 128
            t0 = b0 * 128
            uT = uT_r.next()
            for ci in range(G):
                fc = c0 + ci
                bg, bv = S.ps_next(), S.ps_next()
                pg_, pv_ = S.ps_bank(bg), S.ps_bank(bv)
                for k in range(8):
                    I("pe", "matmul", pg_[:, 0:T + 2], upg[:, k, ci * 128:(ci + 1) * 128], cT[:, k, t0:t0 + T + 2], start=(k == 0), stop=(k == 7),
                      r=[upg[:, k, ci * 128:(ci + 1) * 128], cT[:, k, t0:t0 + T + 2]], w=[pg_[:, 0:T + 2]])
                for k in range(8):
                    I("pe", "matmul", pv_[:, 0:T], upv[:, k, ci * 128:(ci + 1) * 128], cT[:, k, t0 + 1:t0 + 1 + T], start=(k == 0), stop=(k == 7),
                      r=[upv[:, k, ci * 128:(ci + 1) * 128], cT[:, k, t0 + 1:t0 + 1 + T]], w=[pv_[:, 0:T]])
                t1 = t1_r.next()
                ge = ge_r.next()
                I("act", "activation", t1[:, 0:T], pg_[:, 1:T + 1], AF.Identity, scale=convp[:, fc, 1:2], bias=convp[:, fc, 3:4],
                  r=[pg_[:, 0:T + 2], convp], w=[t1[:, 0:T]])
                I("dve", "scalar_tensor_tensor", t1[:, 0:T], pg_[:, 0:T], convp[:, fc, 0:1], t1[:, 0:T], ALU.mult, ALU.add,
                  r=[pg_[:, 0:T + 2], convp, t1[:, 0:T]], w=[t1[:, 0:T]])
                I("dve", "scalar_tensor_tensor", t1[:, 0:T], pg_[:, 2:T + 2], convp[:, fc, 2:3], t1[:, 0:T], ALU.mult, ALU.add,
                  r=[pg_[:, 0:T + 2], convp, t1[:, 0:T]], w=[t1[:, 0:T]])
                I("act", "activation", ge[:, 0:T], t1[:, 0:T], AF.Gelu_apprx_tanh, r=[t1[:, 0:T]], w=[ge[:, 0:T]])
                I("dve", "tensor_tensor", uT[:, ci, 0:T], ge[:, 0:T], pv_[:, 0:T], ALU.mult,
                  r=[ge[:, 0:T], pv_[:, 0:T]], w=[uT[:, ci, 0:T]])
                yield
            for blk in range(nb):
                hb = h[:, b0 + blk, :]
                for half in range(2):
                    hs = slice(half * 512, (half + 1) * 512)
                    bd = S.ps_next()
                    pd = S.ps_bank(bd)
                    for ci in range(G):
                        I("pe", "matmul", pd[:, :], uT[:, ci, blk * 128:(blk + 1) * 128], dn[:, ci, hs], start=(ci == 0), stop=(ci == G - 1),
                          r=[uT[:, ci, blk * 128:(blk + 1) * 128], dn[:, ci, hs]], w=[pd])
                    I("dve", "tensor_tensor", hb[:, hs], pd[:, :], hb[:, hs], ALU.add,
                      r=[pd, hb[:, hs]], w=[hb[:, hs]])
                yield

        banks = [2, 0, 1, 2, 0, 1]
        interleave(make_cT(0))
        NP = len(PASSES)
        for pi in range(NP):
            if pi + 1 < NP:
                nxt_sc = load_ffn_pass(pi + 1, banks[pi + 1])
            else:
                nxt_sc = load_p3_weights(2)
            if pi == NP - 1 and seq + 1 < NSEQ:
                pending_p1_scales += load_p1_weights_a()
            NW = len(WINDOWS)
            if pi == 0:
                interleave(*([make_cT(ww) for ww in range(1, NW)]
                             + [ffn_window(pi, banks[pi], 0), delayed(ffn_window(pi, banks[pi], 1), 3), delayed(ffn_window(pi, banks[pi], 2), 3)]))
                run(nxt_sc)
                interleave(*[ffn_window(pi, banks[pi], ww) for ww in range(3, NW)])
            else:
                for w in range(0, NW, 3):
                    interleave(*[ffn_window(pi, banks[pi], ww) for ww in range(w, w + 3)])
                    if w == 0:
                        run(nxt_sc)
                    if pi == NP - 1:
                        for ww in range(w, w + 3):
                            b0_, nb_ = WINDOWS[ww]
                            for bq in range(b0_, b0_ + nb_):
                                rstd_from(h[:, bq, :], D, RMS_EPS, [h[:, bq, :]], dst=rs_ple[:, bq:bq + 1])
        if debug:
            for b in range(NB):
                S.dma("sp", dbg2[seq, b * 128:(b + 1) * 128, :], h[:, b, :])

        top[0] = A_OFF
        wg, wp = p3_views(2)
        gfin_bc = alloc([D], F32)
        xn3_r = Ring([alloc([D], BF16) for _ in range(4)])
        dT_r = Ring([alloc([8, 128], BF16) for _ in range(8)])
        pf_r = Ring([alloc([256], F32) for _ in range(8)])
        pb_r = Ring([alloc([256], BF16) for _ in range(4)])
        pT_r = Ring([alloc([2, 128], BF16) for _ in range(8)])
        sg_r = Ring([alloc([512], F32) for _ in range(2)])
        tm_r = Ring([alloc([512], F32) for _ in range(2)])
        S.dma("sp", gfin_bc[:, :], gfin_d.partition_broadcast(128))
        if seq + 1 < NSEQ:
            pending_p1_scales += load_p1_weights_b()

        pfs = {}

        def load_p(b):
            if b >= NB:
                return
            pf = pf_r.next()
            S.dma("sp", pf[:, :], p_d[seq, b * 128:(b + 1) * 128, :])
            pfs[b] = pf

        def p3_block(b):
            for _ in range((b // 4) * 2):
                yield
            hb = h[:, b, :]
            dT = dT_r.next()
            xn = xn3_r.next()
            I("dve", "tensor_scalar", xn, hb, rs_ple[:, b:b + 1], None, ALU.mult, r=[hb, rs_ple[:, b:b + 1]], w=[xn])
            pf = pfs[b]
            pbf = pb_r.next()
            I("dve", "tensor_copy", pbf[:, :], pf[:, :], r=[pf], w=[pbf])
            load_p(b + 8)
            yield
            pTx = transposes(xn, 8)
            I("dve", "tensor_copy", dT[:, :, :], pTx, r=[pTx], w=[dT])
            pTp = transposes(pbf, 2)
            pT = pT_r.next()
            I("act", "activation", pT[:, :, :], pTp, AF.Copy, r=[pTp], w=[pT])
            yield
            for half in range(2):
                hs = slice(half * 512, (half + 1) * 512)
                bg, bp = S.ps_next(), S.ps_next()
                pg_, pp_ = S.ps_bank(bg), S.ps_bank(bp)
                for k in range(8):
                    I("pe", "matmul", pg_[:, :], dT[:, k, :], wg[:, k, hs], start=(k == 0), stop=(k == 7),
                      r=[dT[:, k, :], wg[:, k, hs]], w=[pg_])
                for k in range(2):
                    I("pe", "matmul", pp_[:, :], pT[:, k, :], wp[:, k, hs], start=(k == 0), stop=(k == 1),
                      r=[pT[:, k, :], wp[:, k, hs]], w=[pp_])
                sg = sg_r.next()
                tm = tm_r.next()
                I("act", "activation", sg[:, :], pg_[:, :], AF.Sigmoid, r=[pg_], w=[sg])
                I("dve", "tensor_tensor", tm[:, :], pp_[:, :], sg[:, :], ALU.mult, r=[pp_, sg], w=[tm])
                I("dve", "tensor_tensor", hb[:, hs], tm[:, :], hb[:, hs], ALU.add, r=[tm, hb[:, hs]], w=[hb[:, hs]])
                yield

        def p3_final(g4):
            for _ in range(2 * g4 + 4):
                yield
            blks = list(range(4 * g4, 4 * g4 + 4))
            sts = []
            for bq in blks:
                st = stat_ring.next()
                sts.append(st)
                I("act", "activation", junk[:, 0:D], h[:, bq, :], AF.Square, accum_out=st[:, 0:1], r=[h[:, bq, :]], w=[st[:, 0:1], junk[:, 0:D]])
            yield
            for st in sts:
                I("act", "activation", st[:, 1:2], st[:, 0:1], AF.Ln, scale=1.0 / D, bias=RMS_EPS, r=[st[:, 0:1]], w=[st[:, 1:2]])
            for st in sts:
                I("act", "activation", st[:, 2:3], st[:, 1:2], AF.Exp, scale=-0.5, r=[st[:, 1:2]], w=[st[:, 2:3]])
            yield
            for bq, st in zip(blks, sts):
                hb = h[:, bq, :]
                I("dve", "scalar_tensor_tensor", hb, hb, st[:, 2:3], gfin_bc[:, :], ALU.mult, ALU.mult,
                  r=[hb, st[:, 2:3], gfin_bc], w=[hb])
                out_stores.append(S.dma("sp", out_d[seq, bq * 128:(bq + 1) * 128, :], hb))
            yield

        for b in range(8):
            load_p(b)
        interleave(*([p3_block(bb) for bb in range(NB)] + [p3_final(g4) for g4 in range(4)]))

    S.emit_all(out_stores)
    return nc


_NC_CACHE = {}


def _host_layout(inp):
    f = np.float32
    w_in = np.asarray(inp["w_in"][0], f)
    q, k, v = w_in[:, 0:512], w_in[:, 512:640], w_in[:, 640:768]
    zu, zv = w_in[:, 768:1280], w_in[:, 1280:1792]
    k0, k1 = k[:, 0:64], k[:, 64:128]
    w_in_r = np.ascontiguousarray(np.concatenate([q, k0, k0, k1, k1, v, zu, zv], axis=1))

    def pk(vec):
        return np.asarray(vec, f).reshape(8, 128).T

    g_mrg = np.concatenate([np.asarray(inp["g_attn_out"][0], f), np.asarray(inp["g_sgu_out"][0], f)])
    gvec = np.ascontiguousarray(np.concatenate([pk(inp["g_mix"][0]), pk(inp["g_ffn"][0]), pk(inp["g_ple"][0]), pk(g_mrg)], axis=1))
    cw = np.concatenate([np.asarray(inp["conv_w"][0], f), np.asarray(inp["conv_b"][0], f)[None]], axis=0)
    convp = np.ascontiguousarray(cw.reshape(4, NFC, 128).transpose(2, 1, 0).reshape(128, NFC * 4))
    bs8 = np.ascontiguousarray(np.asarray(inp["sgu_b"][0], f))
    ind16 = np.ascontiguousarray(np.tile(np.kron(np.eye(8, dtype=f), np.ones((1, 64), f)), (2, 1)))
    wsT = np.ascontiguousarray(np.asarray(inp["sgu_w"][0], f).transpose(2, 0, 1).reshape(128, 8 * 128))
    s_ = np.arange(128)[:, None, None]
    j_ = np.arange(3)[None, :, None]
    q_ = np.arange(128)[None, None, :]
    dist = np.abs(q_ + 128 - (j_ * 128 + s_)).astype(f)
    dtab = np.ascontiguousarray(np.where(dist <= 128, dist, f(1e6)).astype(f).reshape(128, 384))
    common = {
        "w_in_r": w_in_r,
        "w_out": np.ascontiguousarray(inp["w_out"][0], dtype=f),
        "w_up": np.ascontiguousarray(inp["w_up"][0], dtype=f),
        "w_down": np.ascontiguousarray(inp["w_down"][0], dtype=f),
        "w_pg": np.ascontiguousarray(inp["w_ple_gate"][0], dtype=f),
        "w_pp": np.ascontiguousarray(inp["w_ple_proj"][0], dtype=f),
        "gvec": gvec, "convp": convp, "bs8": bs8, "ind16": ind16, "wsT": wsT, "dtab": dtab,
        "ident": np.eye(128, dtype=f),
        "sink": np.ascontiguousarray(inp["attn_sink"][0], dtype=f),
        "lng": np.ascontiguousarray(inp["sgu_ln_g"][0], dtype=f),
        "lnb": np.ascontiguousarray(inp["sgu_ln_b"][0], dtype=f),
        "gfin": np.ascontiguousarray(inp["g_final"], dtype=f),
    }
    return common


def kernel(**inputs):
    debug = bool(inputs.pop("_debug", False))
    x = np.asarray(inputs["x"], np.float32)
    p = np.asarray(inputs["p"], np.float32)[0]
    common = _host_layout(inputs)
    key = ("nc", debug)
    if key not in _NC_CACHE:
        _NC_CACHE[key] = build_nc(debug)
    nc = _NC_CACHE[key]
    in_maps = []
    for c in range(8):
        m = dict(common)
        m["x"] = np.ascontiguousarray(x[2 * c:2 * c + 2])
        m["p"] = np.ascontiguousarray(p[2 * c:2 * c + 2])
        in_maps.append(m)
    res = run_bass_kernel_spmd(nc, in_maps, core_ids=list(range(8)))
    out = np.concatenate([r["out"] for r in res.results], axis=0).astype(np.float32)
    if debug:
        kernel.dbg = [np.concatenate([r[n] for r in res.results], axis=0) for n in ("dbg1", "dbg2")]
    return out
```

```python
import numpy as np
import concourse.bass as bass
import concourse.mybir as mybir
from concourse.bass_utils import run_bass_kernel_spmd

dt = mybir.dt
AF = mybir.ActivationFunctionType
ALU = mybir.AluOpType
F32, BF16 = dt.float32, dt.bfloat16
ESZ = {dt.float32: 4, dt.bfloat16: 2, dt.uint8: 1, dt.int32: 4, dt.uint32: 4,
       dt.float16: 2, dt.uint16: 2, dt.int16: 2}

GRAN = 64
SB_BYTES = 207 * 1024
NLANE = 8


class Op:
    __slots__ = ("src", "idx", "issuer", "emit", "waits", "signals", "clock", "inc")


class Sched:
    def __init__(self, nc):
        self.nc = nc
        self.engines = ["pe", "act", "dve", "pool", "sp"]
        self.sources = list(self.engines)
        self.lanes = {}
        for q in ("sp", "pool", "act"):
            self.lanes[q] = [f"dma_{q}{i}" for i in range(NLANE)]
            self.sources += self.lanes[q]
        self.sidx = {s: i for i, s in enumerate(self.sources)}
        ns = len(self.sources)
        self.nsrc = ns
        self.src_ops = {s: [] for s in self.sources}
        self.stream = {e: [] for e in self.engines}
        self.know = {e: np.full(ns, -1, np.int64) for e in self.engines}
        ng = SB_BYTES // GRAN
        self.sbW = np.full((ng, ns), -1, np.int64)
        self.sbR = np.full((ng, ns), -1, np.int64)
        self.psW = np.full((8, ns), -1, np.int64)
        self.psR = np.full((8, ns), -1, np.int64)
        self.lane_rr = {q: 0 for q in self.lanes}
        self.big = nc.alloc_sbuf_tensor("big", [128, SB_BYTES], dt.uint8)
        self.psum = nc.alloc_psum_tensor("ps", [128, 8, 512], dt.float32)
        self.ps_rr = 0
        self.n_waits = 0

    def sb_view(self, off, shape, dtype):
        n = int(np.prod(shape))
        nb = n * ESZ[dtype]
        assert off % ESZ[dtype] == 0 and off + nb <= SB_BYTES, (off, nb)
        v = self.big[:, off:off + nb].bitcast(dtype)
        if len(shape) == 2:
            v = v.rearrange("p (a b) -> p a b", a=shape[0])
        elif len(shape) == 3:
            v = v.rearrange("p (a b c) -> p a b c", a=shape[0], b=shape[1])
        return v

    def ps_next(self):
        b = self.ps_rr % 8
        self.ps_rr += 1
        return b

    def ps_bank(self, b, dtype=dt.float32):
        v = self.psum[:, b, :]
        if dtype != dt.float32:
            v = v.bitcast(dtype)
        return v

    @staticmethod
    def _range(ap):
        t = ap.tensor
        esz = ESZ[ap.dtype]
        per_part = int(np.prod(list(t.shape)[1:]))
        col0 = ap.offset % per_part
        ext = 1
        for (st, cnt) in list(ap.ap)[1:]:
            ext += (cnt - 1) * abs(st)
        is_ps = "PSum" in type(t).__name__
        return is_ps, col0 * esz, (col0 + ext) * esz

    def _granules(self, ap):
        is_ps, lo, hi = self._range(ap)
        if is_ps:
            return True, lo // 2048, (hi - 1) // 2048 + 1
        return False, lo // GRAN, (hi - 1) // GRAN + 1

    def _record(self, src, issuer, emit, reads, writes, inc):
        op = Op()
        op.src, op.issuer, op.emit, op.inc = src, issuer, emit, inc
        op.idx = len(self.src_ops[src])
        op.signals = False
        need = np.full(self.nsrc, -1, np.int64)
        rg = [self._granules(ap) for ap in reads]
        wg = [self._granules(ap) for ap in writes]
        wg = wg + [g for g in rg if g[0]]
        rg = [g for g in rg if not g[0]]
        for (ps, a, b) in rg:
            np.maximum(need, self.sbW[a:b].max(axis=0), out=need)
        for (ps, a, b) in wg:
            W, R = (self.psW, self.psR) if ps else (self.sbW, self.sbR)
            np.maximum(need, W[a:b].max(axis=0), out=need)
            np.maximum(need, R[a:b].max(axis=0), out=need)
        si = self.sidx[src]
        if src != issuer and op.idx > 0:
            need[si] = max(need[si], op.idx - 1)
        if src == "pe":
            need[si] = -1
        know = self.know[issuer]
        waits = []
        for s in np.nonzero(need > know)[0]:
            sname = self.sources[s]
            dep = self.src_ops[sname][need[s]]
            dep.signals = True
            waits.append((sname, int(need[s])))
            np.maximum(know, dep.clock, out=know)
        op.waits = waits
        self.n_waits += len(waits)
        clk = know.copy()
        clk[si] = max(clk[si], op.idx)
        op.clock = clk
        for (ps, a, b) in rg:
            self.sbR[a:b, si] = op.idx
        for (ps, a, b) in wg:
            W, R = (self.psW, self.psR) if ps else (self.sbW, self.sbR)
            W[a:b, :] = -1
            W[a:b, si] = op.idx
            R[a:b, :] = -1
        self.src_ops[src].append(op)
        self.stream[issuer].append(op)
        return op

    def op(self, eng, emit, reads=(), writes=()):
        return self._record(eng, eng, emit, list(reads), list(writes), 1)

    def dma(self, q, out, in_, after=(), **kw):
        lane = self.lanes[q][self.lane_rr[q] % NLANE]
        self.lane_rr[q] += 1
        reads = ([in_] if "DRam" not in type(in_.tensor).__name__ else []) + list(after)
        writes = [out] if "DRam" not in type(out.tensor).__name__ else []
        op = self._record(lane, q, lambda e: e.dma_start(out=out, in_=in_, **kw), reads, writes, 16)
        op.signals = True
        return op

    def emit_all(self, final_wait_ops=()):
        nc = self.nc
        cnt = {}
        for s in self.sources:
            c = 0
            arr = []
            for o in self.src_ops[s]:
                if o.signals:
                    c += o.inc
                arr.append(c)
            cnt[s] = arr
        used = [s for s in self.sources if self.src_ops[s]]
        from contextlib import ExitStack
        with ExitStack() as st:
            sems = {s: st.enter_context(nc.semaphore(f"s_{s}")) for s in used}
            block = st.enter_context(nc.Block())

            def run(ename):
                def body(e):
                    for o in self.stream[ename]:
                        for (s, i) in o.waits:
                            e.wait_ge(sems[s], cnt[s][i])
                        ins = o.emit(e)
                        if o.signals:
                            ins.then_inc(sems[o.src], o.inc)
                    if ename == "sp":
                        for o in final_wait_ops:
                            e.wait_ge(sems[o.src], cnt[o.src][o.idx])
                return body

            block.tensor(run("pe"))
            block.scalar(run("act"))
            block.vector(run("dve"))
            block.gpsimd(run("pool"))
            block.sync(run("sp"))


class Ring:
    def __init__(self, views):
        self.v = views
        self.i = 0

    def next(self):
        r = self.v[self.i % len(self.v)]
        self.i += 1
        return r


D = 1024
SEQ = 2048
NB = 16
NSEQ = 2
DFF = 2816
NFC = 22
WIN_COLS = 1920
G_FFN = 4
PASSES = [(0, 4), (4, 4), (8, 4), (12, 4), (16, 3), (19, 3)]
WINDOWS = [(0, 3), (3, 3), (6, 3), (9, 3), (12, 3), (15, 1)]
RMS_EPS = 1e-6
LN_EPS = 1e-5
WBANK = 24576


def build_nc(debug=False):
    nc = bass.Bass("TRN2", target_bir_lowering=False)

    def din(name, shape):
        return nc.dram_tensor(name, list(shape), F32, kind="ExternalInput").ap()

    x_d = din("x", [NSEQ, SEQ, D])
    p_d = din("p", [NSEQ, SEQ, 256])
    win_d = din("w_in_r", [D, WIN_COLS])
    wout_d = din("w_out", [D, D])
    wup_d = din("w_up", [D, 2 * DFF])
    wdn_d = din("w_down", [DFF, D])
    wpg_d = din("w_pg", [D, D])
    wpp_d = din("w_pp", [256, D])
    gvec_d = din("gvec", [128, 32])
    convp_d = din("convp", [128, NFC * 4])
    bs8_d = din("bs8", [8, 128])
    ind_d = din("ind16", [16, 512])
    wsT_d = din("wsT", [128, 8 * 128])
    dtab_d = din("dtab", [128, 3 * 128])
    ident_d = din("ident", [128, 128])
    sink_d = din("sink", [8])
    lng_d = din("lng", [512])
    lnb_d = din("lnb", [512])
    gfin_d = din("gfin", [D])
    out_d = nc.dram_tensor("out", [NSEQ, SEQ, D], F32, kind="ExternalOutput").ap()
    if debug:
        dbg1 = nc.dram_tensor("dbg1", [NSEQ, SEQ, D], F32, kind="ExternalOutput").ap()
        dbg2 = nc.dram_tensor("dbg2", [NSEQ, SEQ, D], F32, kind="ExternalOutput").ap()

    S = Sched(nc)
    top = [0]

    def I(eng, fname, *args, r=(), w=(), **kw):
        return S.op(eng, lambda e: getattr(e, fname)(*args, **kw), reads=r, writes=w)

    def alloc(shape, dtype, align=64):
        n = int(np.prod(shape)) * ESZ[dtype]
        off = (top[0] + align - 1) // align * align
        top[0] = off + n
        assert top[0] <= SB_BYTES, f"SBUF overflow {top[0]}"
        return S.sb_view(off, shape, dtype)

    h = alloc([NB, D], F32)
    W_OFF = top[0]
    top[0] += 3 * WBANK
    ident = alloc([128], BF16)
    gvec = alloc([32], F32)
    convp = alloc([NFC, 4], F32)
    expb = alloc([3, 2, 512], BF16)
    bs2x = alloc([128], BF16)
    ind16 = alloc([512], BF16)
    sinkb = alloc([8], F32)
    expsink = alloc([8], F32)
    lng_bc = alloc([512], F32)
    lnb_bc = alloc([512], F32)
    wsT = alloc([8, 128], BF16)
    junk = alloc([1024], BF16)
    rs_mix = alloc([16], F32)
    rs_ffn = alloc([16], F32)
    rs_ple = alloc([16], F32)
    stat_ring = Ring([alloc([16], F32) for _ in range(46)])
    A_OFF = top[0]

    win = S.sb_view(W_OFF, [8, WIN_COLS], BF16)
    wout = S.sb_view(W_OFF + 8 * WIN_COLS * 2, [8, D], BF16)

    def ffn_bank_views(bank, G):
        o = W_OFF + bank * WBANK
        upg = S.sb_view(o, [8, G * 128], BF16)
        upv = S.sb_view(o + 8 * G * 128 * 2, [8, G * 128], BF16)
        dn = S.sb_view(o + 16 * G * 128 * 2, [G, D], BF16)
        return upg, upv, dn

    def p3_views(bank):
        o = W_OFF + bank * WBANK
        return S.sb_view(o, [8, D], BF16), S.sb_view(o + 8 * D * 2, [2, D], BF16)

    for b in range(6):
        S.dma("sp", h[:, b, :], x_d[0, b * 128:(b + 1) * 128, :])
    S.dma("pool", ident[:, :], ident_d)
    S.dma("sp", gvec[:, :], gvec_d)
    S.dma("sp", convp[:, :, :], convp_d.rearrange("p (c f) -> p c f", f=4))
    S.dma("sp", sinkb[:, :], sink_d.partition_broadcast(128))
    S.dma("sp", lng_bc[:, :], lng_d.partition_broadcast(128))
    S.dma("sp", lnb_bc[:, :], lnb_d.partition_broadcast(128))
    S.dma("pool", wsT[:, :, :], wsT_d.rearrange("p (h t) -> p h t", h=8))
    I("act", "activation", expsink[:, :], sinkb[:, :], AF.Exp, r=[sinkb], w=[expsink])
    g_mix, g_ffn, g_ple, g_mrg = gvec[:, 0:8], gvec[:, 8:16], gvec[:, 16:24], gvec[:, 24:32]
    top[0] = A_OFF
    dtab = alloc([3, 128], F32)
    bsf = alloc([128], F32)
    bsd = alloc([128], F32)
    bslo = alloc([128], BF16)
    S.dma("sp", dtab[:, :, :], dtab_d.rearrange("p (j q) -> p j q", j=3))
    S.dma("sp", bsf[0:8, :], bs8_d)
    S.dma("sp", bsf[8:16, :], bs8_d)
    S.dma("pool", ind16[0:16, :], ind_d)
    for j in range(3):
        for g in range(2):
            for par in range(2):
                for hh in range(2):
                    hd = 4 * g + 2 * hh + par
                    c0 = par * 256 + hh * 128
                    I("act", "activation", expb[:, j, g, c0:c0 + 128], dtab[:, j, :], AF.Exp, scale=-(2.0 ** (-(hd + 1))),
                      r=[dtab[:, j, :]], w=[expb[:, j, g, c0:c0 + 128]])
    I("dve", "tensor_copy", bs2x[0:16, :], bsf[0:16, :], r=[bsf], w=[bs2x])
    I("dve", "tensor_tensor", bsd[0:16, :], bsf[0:16, :], bs2x[0:16, :], ALU.subtract, r=[bsf, bs2x], w=[bsd])
    I("dve", "tensor_copy", bslo[0:16, :], bsd[0:16, :], r=[bsd], w=[bslo])
    S.dma("sp", bs2x[8:16, :], bslo[8:16, :])

    def scale_rows(wv, gcol):
        I("dve", "tensor_scalar", wv, wv, gcol, None, ALU.mult, r=[wv, gcol], w=[wv])

    def wdma(dst, src, after=()):
        S.dma("pool", dst, src, after=after)

    def load_p1_weights_a():
        for kk in range(0, 6, 2):
            wdma(win[:, kk:kk + 2, :], win_d[kk * 128:(kk + 2) * 128, :].rearrange("(k p) n -> p k n", p=128))
        return [lambda k=k: scale_rows(win[:, k, :], g_mix[:, k:k + 1]) for k in range(6)]

    def load_p1_weights_b():
        wdma(win[:, 6:8, :], win_d[6 * 128:8 * 128, :].rearrange("(k p) n -> p k n", p=128))
        for kk in range(0, 8, 4):
            wdma(wout[:, kk:kk + 4, :], wout_d[kk * 128:(kk + 4) * 128, :].rearrange("(k p) n -> p k n", p=128))
        return ([lambda k=k: scale_rows(win[:, k, :], g_mix[:, k:k + 1]) for k in (6, 7)]
                + [lambda k=k: scale_rows(wout[:, k, :], g_mrg[:, k:k + 1]) for k in range(8)])

    def load_p1_weights_startup():
        groups = []
        for (c0, c1) in ((896, 1920), (0, 896)):
            for kk in range(0, 8, 4):
                wdma(win[:, kk:kk + 4, c0:c1], win_d[kk * 128:(kk + 4) * 128, c0:c1].rearrange("(k p) n -> p k n", p=128))
        groups.append([lambda k=k: scale_rows(win[:, k, 896:1920], g_mix[:, k:k + 1]) for k in range(8)])
        groups.append([lambda k=k: scale_rows(win[:, k, 0:896], g_mix[:, k:k + 1]) for k in range(8)])
        groups.append([lambda k=k: scale_rows(wout[:, k, :], g_mrg[:, k:k + 1]) for k in range(8)])
        return groups

    def load_wout_startup():
        for kk in range(0, 8, 4):
            wdma(wout[:, kk:kk + 4, :], wout_d[kk * 128:(kk + 4) * 128, :].rearrange("(k p) n -> p k n", p=128),
                 after=[win[:, 7, 0:64]])

    def load_ffn_pass(pi, bank, after=()):
        c0, G = PASSES[pi]
        upg, upv, dn = ffn_bank_views(bank, G)
        wdma(upg[:, :, :], wup_d[:, c0 * 128:(c0 + G) * 128].rearrange("(k p) n -> p k n", p=128), after=after)
        wdma(upv[:, :, :], wup_d[:, DFF + c0 * 128:DFF + (c0 + G) * 128].rearrange("(k p) n -> p k n", p=128), after=after)
        wdma(dn[:, :, :], wdn_d[c0 * 128:(c0 + G) * 128, :].rearrange("(c p) n -> p c n", p=128), after=after)
        return ([lambda k=k: scale_rows(upg[:, k, :], g_ffn[:, k:k + 1]) for k in range(8)]
                + [lambda k=k: scale_rows(upv[:, k, :], g_ffn[:, k:k + 1]) for k in range(8)])

    def load_p3_weights(bank):
        wg, wp = p3_views(bank)
        for kk in range(0, 8, 4):
            wdma(wg[:, kk:kk + 4, :], wpg_d[kk * 128:(kk + 4) * 128, :].rearrange("(k p) n -> p k n", p=128))
        wdma(wp[:, :, :], wpp_d.rearrange("(k p) n -> p k n", p=128))
        return [lambda k=k: scale_rows(wg[:, k, :], g_ple[:, k:k + 1]) for k in range(8)]

    def run(thunks):
        for t in thunks:
            t()

    def interleave(*gens):
        gens = [g for g in gens if g is not None]
        while gens:
            for g in list(gens):
                try:
                    next(g)
                except StopIteration:
                    gens.remove(g)

    def chain(*gens):
        for g in gens:
            if g is not None:
                yield from g

    def rstd_from(src, n, eps, reads, dst=None):
        st = stat_ring.next()
        ss, lnv, rs = st[:, 0:1], st[:, 1:2], st[:, 2:3]
        if dst is not None:
            rs = dst
        I("act", "activation", junk[:, 0:n], src, AF.Square, accum_out=ss, r=reads, w=[ss, junk[:, 0:n]])
        I("act", "activation", lnv, ss, AF.Ln, scale=1.0 / n, bias=eps, r=[ss], w=[lnv])
        I("act", "activation", rs, lnv, AF.Exp, scale=-0.5, r=[lnv], w=[rs])
        return rs

    def transposes(src, nchunks):
        b = S.ps_next()
        pb = S.ps_bank(b, BF16)
        for c in range(nchunks):
            I("pe", "transpose", pb[:, c * 128:(c + 1) * 128], src[:, c * 128:(c + 1) * 128], ident[:, :],
                 r=[src[:, c * 128:(c + 1) * 128], ident], w=[pb[:, c * 128:(c + 1) * 128]])
        return pb[:, 0:nchunks * 128].rearrange("p (c t) -> p c t", c=nchunks)

    def norm_transpose(hb, xn, dst, rs=None, evac="act"):
        if rs is None:
            rs = rstd_from(hb, D, RMS_EPS, [hb])
        I("dve", "tensor_scalar", xn, hb, rs, None, ALU.mult, r=[hb, rs], w=[xn])
        pT = transposes(xn, 8)
        if evac == "act":
            I("act", "activation", dst, pT, AF.Copy, r=[pT], w=[dst])
        else:
            I("dve", "tensor_copy", dst, pT, r=[pT], w=[dst])

    out_stores = []

    pending_p1_scales = []
    for seq in range(NSEQ):
        top[0] = A_OFF
        kz = [alloc([2, 768], BF16) for _ in range(2)]
        vaug = alloc([6, 2, 65], BF16)
        qT = alloc([4, 768], BF16)
        NSG = 5
        sgT = alloc([4, NSG * 128], BF16)
        XG_OFF = (top[0] + 63) // 64 * 64
        xg_bytes = [alloc([512], F32) for _ in range(2)]
        xg_r = Ring([(v.bitcast(BF16), v) for v in xg_bytes])
        aT = alloc([8, 256], BF16)
        u_r = Ring([alloc([512], BF16) for _ in range(2)])
        t2_r = Ring([alloc([512], BF16) for _ in range(2)])
        n_r = Ring([alloc([512], BF16) for _ in range(2)])
        sgo_r = Ring([alloc([512], BF16) for _ in range(2)])
        e_r = Ring([alloc([3, 512], BF16) for _ in range(4)])
        ao_r = Ring([alloc([512], BF16) for _ in range(2)])
        aoT_r = Ring([alloc([4, 128], BF16) for _ in range(2)])
        rs_ring = [None] * NSG
        p1_top = top[0]

        def tslot(b):
            return ((b // 2) % 3) * 256 + (b % 2) * 128

        def vslot(b):
            return ((b // 2) % 3) * 2 + (b % 2)

        if seq == 0:
            w_groups = load_p1_weights_startup()
            run(w_groups[0])
        else:
            run(pending_p1_scales)
            pending_p1_scales = []

        I("dve", "memset", kz[0][64:128, :, :], 0.0, w=[kz[0]])
        I("dve", "memset", kz[1][0:64, :, :], 0.0, w=[kz[1]])
        I("dve", "memset", vaug[:, :, :, 64:65], 1.0, w=[vaug])

        def load_x(b):
            S.dma("sp", h[:, b, :], x_d[seq, b * 128:(b + 1) * 128, :])

        def stageA_block(b):
            bb = b % 2
            hb = h[:, b, :]
            xn, gv = xg_r.next()
            I("dve", "tensor_scalar", xn, hb, rs_mix[:, b:b + 1], None, ALU.mult, r=[hb, rs_mix[:, b:b + 1]], w=[xn])
            yield
            pTx = transposes(xn, 8)
            I("dve", "tensor_copy", aT[:, :, bb * 128:(bb + 1) * 128], pTx, r=[pTx], w=[aT[:, :, bb * 128:(bb + 1) * 128]])
            yield
            lhs = lambda k: aT[:, k, bb * 128:(bb + 1) * 128]
            bv, bu, bz = S.ps_next(), S.ps_next(), S.ps_next()
            pv_, pu_, pz_ = S.ps_bank(bv), S.ps_bank(bu), S.ps_bank(bz)
            for (pb, c0, c1) in ((pz_[:, :], 1408, 1920), (pu_[:, :], 896, 1408), (pv_[:, 0:128], 768, 896)):
                for k in range(8):
                    I("pe", "matmul", pb, lhs(k), win[:, k, c0:c1], start=(k == 0), stop=(k == 7),
                      r=[lhs(k), win[:, k, c0:c1]], w=[pb])
            u = u_r.next()
            I("act", "activation", gv[:, :], pz_[:, :], AF.Gelu_apprx_tanh, r=[pz_], w=[gv])
            I("act", "activation", u[:, :], pu_[:, :], AF.Gelu_apprx_tanh, r=[pu_], w=[u])
            vs = vslot(b)
            I("act", "activation", vaug[:, vs, :, 0:64], pv_[:, 0:128].rearrange("p (g d) -> p g d", g=2), AF.Copy,
              r=[pv_[:, 0:128]], w=[vaug[:, vs, :, 0:64]])
            yield
            st = stat_ring.next()
            st2 = stat_ring.next()
            bst, mv = st[:, 0:6], st[:, 8:10]
            lnv, rln = st2[:, 0:1], st2[:, 1:2]
            I("dve", "bn_stats", bst, gv[:, :], r=[gv], w=[bst])
            I("dve", "bn_aggr", mv, bst, r=[bst], w=[mv])
            I("act", "activation", lnv, mv[:, 1:2], AF.Ln, bias=LN_EPS, r=[mv], w=[lnv])
            I("act", "activation", rln, lnv, AF.Exp, scale=-0.5, r=[lnv], w=[rln])
            t2 = t2_r.next()
            nn = n_r.next()
            I("dve", "scalar_tensor_tensor", t2[:, :], gv[:, :], mv[:, 0:1], lng_bc[:, :], ALU.subtract, ALU.mult,
              r=[gv, mv, lng_bc], w=[t2])
            I("dve", "scalar_tensor_tensor", nn[:, :], t2[:, :], rln, lnb_bc[:, :], ALU.mult, ALU.add,
              r=[t2, rln, lnb_bc], w=[nn])
            yield
            bs_ = S.ps_next()
            psb = S.ps_bank(bs_)
            for hh in range(8):
                cs = slice(hh * 64, (hh + 1) * 64)
                I("pe", "matmul", psb[:, cs], wsT[:, hh, :], nn[:, cs], start=(hh == 0), stop=False, skip_group_check=True,
                  r=[wsT[:, hh, :], nn[:, cs]], w=[psb[:, cs]])
            I("pe", "matmul", psb[:, :], bs2x[0:16, :], ind16[0:16, :], start=False, stop=True, skip_group_check=True,
              r=[bs2x, ind16], w=[psb])
            sgo = sgo_r.next()
            I("dve", "tensor_tensor", sgo[:, :], psb[:, :], u[:, :], ALU.mult, r=[psb, u], w=[sgo])
            rs_ring[b % NSG] = rstd_from(sgo[:, :], 512, RMS_EPS, [sgo])
            yield
            pT = transposes(sgo, 4)
            slot = b % NSG
            I("dve", "tensor_copy", sgT[:, :, slot * 128:(slot + 1) * 128], pT,
              r=[pT], w=[sgT[:, :, slot * 128:(slot + 1) * 128]])
            yield

        def stageA_qk(st_i):
            so = (st_i % 3) * 256
            for i in range(3):
                b_ = S.ps_next()
                pb = S.ps_bank(b_)
                for cc in range(2):
                    c = 2 * i + cc
                    for k in range(8):
                        I("pe", "matmul", pb[:, cc * 256:(cc + 1) * 256], win[:, k, c * 128:(c + 1) * 128], aT[:, k, :], start=(k == 0), stop=(k == 7),
                          r=[win[:, k, c * 128:(c + 1) * 128], aT[:, k, :]], w=[pb[:, cc * 256:(cc + 1) * 256]])
                src3 = pb[:, 0:512].rearrange("p (c t) -> p c t", c=2)
                if i < 2:
                    I("dve", "tensor_copy", qT[:, 2 * i:2 * i + 2, so:so + 256], src3,
                      r=[pb[:, 0:512]], w=[qT[:, 2 * i:2 * i + 2, so:so + 256]])
                else:
                    for par in range(2):
                        ps_ = slice(par * 64, (par + 1) * 64)
                        I("act", "activation", kz[par][ps_, :, so:so + 256], src3[ps_, :, :], AF.Copy,
                          r=[pb[:, 0:512]], w=[kz[par][:, :, so:so + 256]])
                yield

        def stageB(n):
            hb = h[:, n, :]
            qo = tslot(n)
            js = [j for j in range(3) if 0 <= n - 1 + j < NB]
            ao = ao_r.next()
            st = stat_ring.next()
            den, rden = st[:, 0:8], st[:, 8:16]
            Es = [e_r.next(), e_r.next()]
            for g in range(2):
                E = Es[g]
                for j in js:
                    kb = n - 1 + j
                    ko = tslot(kb)
                    b_ = S.ps_next()
                    pb = S.ps_bank(b_)
                    for par in range(2):
                        I("pe", "matmul", pb[:, par * 256:(par + 1) * 256], kz[par][:, g, ko:ko + 128],
                          qT[:, 2 * g:2 * g + 2, qo:qo + 128], start=True, stop=True,
                          r=[kz[par][:, g, ko:ko + 128], qT[:, 2 * g:2 * g + 2, qo:qo + 128]],
                          w=[pb[:, par * 256:(par + 1) * 256]])
                    I("act", "activation", E[:, j, :], pb[:, :], AF.Exp, scale=0.125, r=[pb], w=[E[:, j, :]])
                    I("dve", "tensor_tensor", E[:, j, :], E[:, j, :], expb[:, j, g, :], ALU.mult,
                      r=[E[:, j, :], expb[:, j, g, :]], w=[E[:, j, :]])
            yield
            for g in range(2):
                E = Es[g]
                bp = S.ps_next()
                pvb = S.ps_bank(bp)
                for idx in range(4):
                    hh, par = idx // 2, idx % 2
                    c0 = par * 256 + hh * 128
                    for j in js:
                        kb = n - 1 + j
                        I("pe", "matmul", pvb[:, idx * 65:(idx + 1) * 65], E[:, j, c0:c0 + 128], vaug[:, vslot(kb), g, :],
                          start=(j == js[0]), stop=(j == js[-1]),
                          r=[E[:, j, c0:c0 + 128], vaug[:, vslot(kb), g, :]], w=[pvb[:, idx * 65:(idx + 1) * 65]])
                pv3 = pvb[:, 0:260].rearrange("p (h d) -> p h d", h=4)
                dg, rg_ = den[:, 4 * g:4 * g + 4], rden[:, 4 * g:4 * g + 4]
                I("dve", "tensor_tensor", dg, pv3[:, :, 64], expsink[:, 4 * g:4 * g + 4], ALU.add,
                  r=[pvb[:, 0:260], expsink], w=[dg])
                I("dve", "reciprocal", rg_, dg, r=[dg], w=[rg_])
                ao3 = ao[:, g * 256:(g + 1) * 256].rearrange("p (h d) -> p h d", h=4)
                I("dve", "tensor_tensor", ao3, pv3[:, :, 0:64], rg_.unsqueeze(2).to_broadcast([128, 4, 64]), ALU.mult,
                  r=[pvb[:, 0:260], rg_], w=[ao[:, g * 256:(g + 1) * 256]])
            yield
            r_a = rstd_from(ao[:, :], 512, RMS_EPS, [ao])
            r_s = rs_ring[n % NSG]
            pT = transposes(ao, 4)
            aoT = aoT_r.next()
            I("dve", "tensor_copy", aoT[:, :, :], pT, r=[pT], w=[aoT])
            yield
            slot = n % NSG
            for half in range(2):
                hs = slice(half * 512, (half + 1) * 512)
                ba, bb_ = S.ps_next(), S.ps_next()
                pa, pb2 = S.ps_bank(ba), S.ps_bank(bb_)
                for c in range(4):
                    I("pe", "matmul", pb2[:, :], sgT[:, c, slot * 128:(slot + 1) * 128], wout[:, 4 + c, hs], start=(c == 0), stop=(c == 3),
                      r=[sgT[:, c, slot * 128:(slot + 1) * 128], wout[:, 4 + c, hs]], w=[pb2])
                for c in range(4):
                    I("pe", "matmul", pa[:, :], aoT[:, c, :], wout[:, c, hs], start=(c == 0), stop=(c == 3),
                      r=[aoT[:, c, :], wout[:, c, hs]], w=[pa])
                I("dve", "scalar_tensor_tensor", hb[:, hs], pb2[:, :], r_s, hb[:, hs], ALU.mult, ALU.add,
                  r=[pb2, r_s, hb[:, hs]], w=[hb[:, hs]])
                I("dve", "scalar_tensor_tensor", hb[:, hs], pa[:, :], r_a, hb[:, hs], ALU.mult, ALU.add,
                  r=[pa, r_a, hb[:, hs]], w=[hb[:, hs]])
                yield
            rstd_from(hb, D, RMS_EPS, [hb], dst=rs_ffn[:, n:n + 1])
            yield

        def delayed(gen, n):
            for _ in range(n):
                yield
            yield from gen

        def streamsA(st_i):
            if st_i >= 8:
                return [None, None, None]
            return [stageA_block(2 * st_i), stageA_block(2 * st_i + 1), delayed(stageA_qk(st_i), 2)]

        def streamR(st_i):
            if st_i >= 8:
                return
            for bq in (2 * st_i, 2 * st_i + 1):
                rstd_from(h[:, bq, :], D, RMS_EPS, [h[:, bq, :]], dst=rs_mix[:, bq:bq + 1])
                yield

        def streamsB(st_i):
            gs = [stageB(n) if n >= 0 else None for n in (2 * st_i - 1, 2 * st_i)]
            return gs

        if seq > 0:
            for b in range(6):
                load_x(b)
        interleave(streamR(0), streamR(1))
        gA0 = streamsA(0)
        if seq == 0:
            for g_ in gA0:
                next(g_)
            run(w_groups[1])
            load_wout_startup()
            ffn_sc = load_ffn_pass(0, 2, after=[win[:, 7, 0:64]])
            run(w_groups[2])
        interleave(*gA0)
        if seq > 0:
            ffn_sc = load_ffn_pass(0, 2, after=[aT[:, 0, 0:8]])
        for st_i in range(8):
            for b in (2 * st_i + 6, 2 * st_i + 7):
                if b < NB:
                    load_x(b)
            extra = []
            if st_i == 7:
                e_r.v += [S.sb_view(XG_OFF, [3, 512], BF16), S.sb_view(XG_OFF + 3072, [3, 512], BF16)]
                ao_r.v += [u_r.v[0]]
                aoT_r.v += [u_r.v[1].rearrange("p (c t) -> p c t", c=4)]
                extra = [stageB(NB - 1)]
            interleave(*(streamsA(st_i + 1) + streamsB(st_i) + extra + [streamR(st_i + 2)]))
            if st_i == 1:
                run(ffn_sc)
        if debug:
            for b in range(NB):
                S.dma("sp", dbg1[seq, b * 128:(b + 1) * 128, :], h[:, b, :])

        top[0] = A_OFF
        cT = alloc([8, SEQ + 2], BF16)
        uT_r = Ring([alloc([G_FFN, 384], BF16) for _ in range(3)])
        t1_r = Ring([alloc([384], F32) for _ in range(2)])
        ge_r = Ring([alloc([384], BF16) for _ in range(2)])
        XN2_OFF = (top[0] + 63) // 64 * 64
        xn2_r = Ring([alloc([D], BF16) for _ in range(2)])
        I("dve", "memset", cT[:, :, 0:1], 0.0, w=[cT[:, :, 0:1]])
        I("dve", "memset", cT[:, :, SEQ + 1:SEQ + 2], 0.0, w=[cT[:, :, SEQ + 1:SEQ + 2]])

        def make_cT(w):
            if w >= len(WINDOWS):
                return
            b0, nb = WINDOWS[w]
            for b in range(b0, b0 + nb):
                norm_transpose(h[:, b, :], xn2_r.next(), cT[:, :, 1 + b * 128:1 + (b + 1) * 128], rs=rs_ffn[:, b:b + 1])
                yield

        def ffn_window(pi, bank, w, uT=None):
            c0, G = PASSES[pi]
            upg, upv, dn = ffn_bank_views(bank, G)
            b0, nb = WINDOWS[w]
            T = nb * 128
            t0 = b0 * 128
            if uT is None:
                uT = uT_r.next()
            for ci in range(G):
                fc = c0 + ci
                bg, bv = S.ps_next(), S.ps_next()
                pg_, pv_ = S.ps_bank(bg), S.ps_bank(bv)
                for k in range(8):
                    I("pe", "matmul", pg_[:, 0:T + 2], upg[:, k, ci * 128:(ci + 1) * 128], cT[:, k, t0:t0 + T + 2], start=(k == 0), stop=(k == 7),
                      r=[upg[:, k, ci * 128:(ci + 1) * 128], cT[:, k, t0:t0 + T + 2]], w=[pg_[:, 0:T + 2]])
                for k in range(8):
                    I("pe", "matmul", pv_[:, 0:T], upv[:, k, ci * 128:(ci + 1) * 128], cT[:, k, t0 + 1:t0 + 1 + T], start=(k == 0), stop=(k == 7),
                      r=[upv[:, k, ci * 128:(ci + 1) * 128], cT[:, k, t0 + 1:t0 + 1 + T]], w=[pv_[:, 0:T]])
                t1 = t1_r.next()
                ge = ge_r.next()
                I("act", "activation", t1[:, 0:T], pg_[:, 1:T + 1], AF.Identity, scale=convp[:, fc, 1:2], bias=convp[:, fc, 3:4],
                  r=[pg_[:, 0:T + 2], convp], w=[t1[:, 0:T]])
                I("dve", "scalar_tensor_tensor", t1[:, 0:T], pg_[:, 0:T], convp[:, fc, 0:1], t1[:, 0:T], ALU.mult, ALU.add,
                  r=[pg_[:, 0:T + 2], convp, t1[:, 0:T]], w=[t1[:, 0:T]])
                I("dve", "scalar_tensor_tensor", t1[:, 0:T], pg_[:, 2:T + 2], convp[:, fc, 2:3], t1[:, 0:T], ALU.mult, ALU.add,
                  r=[pg_[:, 0:T + 2], convp, t1[:, 0:T]], w=[t1[:, 0:T]])
                I("act", "activation", ge[:, 0:T], t1[:, 0:T], AF.Gelu_apprx_tanh, r=[t1[:, 0:T]], w=[ge[:, 0:T]])
                I("dve", "tensor_tensor", uT[:, ci, 0:T], ge[:, 0:T], pv_[:, 0:T], ALU.mult,
                  r=[ge[:, 0:T], pv_[:, 0:T]], w=[uT[:, ci, 0:T]])
                yield
            for blk in range(nb):
                hb = h[:, b0 + blk, :]
                for half in range(2):
                    hs = slice(half * 512, (half + 1) * 512)
                    bd = S.ps_next()
                    pd = S.ps_bank(bd)
                    for ci in range(G):
                        I("pe", "matmul", pd[:, :], uT[:, ci, blk * 128:(blk + 1) * 128], dn[:, ci, hs], start=(ci == 0), stop=(ci == G - 1),
                          r=[uT[:, ci, blk * 128:(blk + 1) * 128], dn[:, ci, hs]], w=[pd])
                    I("dve", "tensor_tensor", hb[:, hs], pd[:, :], hb[:, hs], ALU.add,
                      r=[pd, hb[:, hs]], w=[hb[:, hs]])
                yield

        banks = [2, 0, 1, 2, 0, 1]
        interleave(make_cT(0))
        NP = len(PASSES)
        def rolling(items, k):
            free = list(range(k))
            active = []
            idx = 0
            while idx < len(items) or active:
                while idx < len(items) and free:
                    mk, on_start, on_done = items[idx]
                    idx += 1
                    slot = free.pop(0)
                    if on_start:
                        on_start()
                    active.append((mk(slot), slot, on_done))
                for ent in list(active):
                    try:
                        next(ent[0])
                    except StopIteration:
                        active.remove(ent)
                        free.append(ent[1])
                        if ent[2]:
                            ent[2]()

        NW = len(WINDOWS)
        sc_next = {}
        sc_next[1] = load_ffn_pass(1, banks[1])
        interleave(*([make_cT(ww) for ww in range(1, NW)]
                     + [ffn_window(0, banks[0], 0), delayed(ffn_window(0, banks[0], 1), 3), delayed(ffn_window(0, banks[0], 2), 3)]))
        run(sc_next[1])
        interleave(*[ffn_window(0, banks[0], ww) for ww in range(3, NW)])
        KROLL = 4
        uT_slots = list(uT_r.v) + [S.sb_view(XN2_OFF, [G_FFN, 384], BF16)]
        items = []
        for pi in range(1, NP):
            for w in range(NW):
                def mk(slot, pi=pi, w=w):
                    return ffn_window(pi, banks[pi], w, uT=uT_slots[slot])
                on_start = None
                on_done = None
                if w == 0:
                    def on_start(pi=pi):
                        if pi + 1 < NP:
                            sc_next[pi + 1] = load_ffn_pass(pi + 1, banks[pi + 1])
                        else:
                            sc_next[pi + 1] = load_p3_weights(2)
                def on_done(pi=pi, w=w):
                    if w == 2:
                        run(sc_next[pi + 1])
                        if pi == NP - 1 and seq + 1 < NSEQ:
                            pending_p1_scales.extend(load_p1_weights_a())
                    if pi == NP - 1:
                        b0_, nb_ = WINDOWS[w]
                        for bq in range(b0_, b0_ + nb_):
                            rstd_from(h[:, bq, :], D, RMS_EPS, [h[:, bq, :]], dst=rs_ple[:, bq:bq + 1])
                items.append((mk, on_start, on_done))
        rolling(items, KROLL)
        if debug:
            for b in range(NB):
                S.dma("sp", dbg2[seq, b * 128:(b + 1) * 128, :], h[:, b, :])

        top[0] = A_OFF
        wg, wp = p3_views(2)
        gfin_bc = alloc([D], F32)
        xn3_r = Ring([alloc([D], BF16) for _ in range(4)])
        dT_r = Ring([alloc([8, 128], BF16) for _ in range(8)])
        pf_r = Ring([alloc([256], F32) for _ in range(8)])
        pb_r = Ring([alloc([256], BF16) for _ in range(4)])
        pT_r = Ring([alloc([2, 128], BF16) for _ in range(8)])
        sg_r = Ring([alloc([512], F32) for _ in range(2)])
        tm_r = Ring([alloc([512], F32) for _ in range(2)])
        S.dma("sp", gfin_bc[:, :], gfin_d.partition_broadcast(128))
        if seq + 1 < NSEQ:
            pending_p1_scales += load_p1_weights_b()

        pfs = {}

        def load_p(b):
            if b >= NB:
                return
            pf = pf_r.next()
            S.dma("sp", pf[:, :], p_d[seq, b * 128:(b + 1) * 128, :])
            pfs[b] = pf

        def p3_block(b):
            for _ in range((b // 4) * 2):
                yield
            hb = h[:, b, :]
            dT = dT_r.next()
            xn = xn3_r.next()
            I("dve", "tensor_scalar", xn, hb, rs_ple[:, b:b + 1], None, ALU.mult, r=[hb, rs_ple[:, b:b + 1]], w=[xn])
            pf = pfs[b]
            pbf = pb_r.next()
            I("dve", "tensor_copy", pbf[:, :], pf[:, :], r=[pf], w=[pbf])
            load_p(b + 8)
            yield
            pTx = transposes(xn, 8)
            I("dve", "tensor_copy", dT[:, :, :], pTx, r=[pTx], w=[dT])
            pTp = transposes(pbf, 2)
            pT = pT_r.next()
            I("act", "activation", pT[:, :, :], pTp, AF.Copy, r=[pTp], w=[pT])
            yield
            for half in range(2):
                hs = slice(half * 512, (half + 1) * 512)
                bg, bp = S.ps_next(), S.ps_next()
                pg_, pp_ = S.ps_bank(bg), S.ps_bank(bp)
                for k in range(8):
                    I("pe", "matmul", pg_[:, :], dT[:, k, :], wg[:, k, hs], start=(k == 0), stop=(k == 7),
                      r=[dT[:, k, :], wg[:, k, hs]], w=[pg_])
                for k in range(2):
                    I("pe", "matmul", pp_[:, :], pT[:, k, :], wp[:, k, hs], start=(k == 0), stop=(k == 1),
                      r=[pT[:, k, :], wp[:, k, hs]], w=[pp_])
                sg = sg_r.next()
                tm = tm_r.next()
                I("act", "activation", sg[:, :], pg_[:, :], AF.Sigmoid, r=[pg_], w=[sg])
                I("dve", "tensor_tensor", tm[:, :], pp_[:, :], sg[:, :], ALU.mult, r=[pp_, sg], w=[tm])
                I("dve", "tensor_tensor", hb[:, hs], tm[:, :], hb[:, hs], ALU.add, r=[tm, hb[:, hs]], w=[hb[:, hs]])
                yield

        def p3_final(g4):
            for _ in range(2 * g4 + 4):
                yield
            blks = list(range(4 * g4, 4 * g4 + 4))
            sts = []
            for bq in blks:
                st = stat_ring.next()
                sts.append(st)
                I("act", "activation", junk[:, 0:D], h[:, bq, :], AF.Square, accum_out=st[:, 0:1], r=[h[:, bq, :]], w=[st[:, 0:1], junk[:, 0:D]])
            yield
            for st in sts:
                I("act", "activation", st[:, 1:2], st[:, 0:1], AF.Ln, scale=1.0 / D, bias=RMS_EPS, r=[st[:, 0:1]], w=[st[:, 1:2]])
            for st in sts:
                I("act", "activation", st[:, 2:3], st[:, 1:2], AF.Exp, scale=-0.5, r=[st[:, 1:2]], w=[st[:, 2:3]])
            yield
            for bq, st in zip(blks, sts):
                hb = h[:, bq, :]
                I("dve", "scalar_tensor_tensor", hb, hb, st[:, 2:3], gfin_bc[:, :], ALU.mult, ALU.mult,
                  r=[hb, st[:, 2:3], gfin_bc], w=[hb])
                out_stores.append(S.dma("sp", out_d[seq, bq * 128:(bq + 1) * 128, :], hb))
            yield

        for b in range(8):
            load_p(b)
        interleave(*([p3_block(bb) for bb in range(NB)] + [p3_final(g4) for g4 in range(4)]))

    S.emit_all(out_stores)
    return nc


_NC_CACHE = {}


def _host_layout(inp):
    f = np.float32
    w_in = np.asarray(inp["w_in"][0], f)
    q, k, v = w_in[:, 0:512], w_in[:, 512:640], w_in[:, 640:768]
    zu, zv = w_in[:, 768:1280], w_in[:, 1280:1792]
    k0, k1 = k[:, 0:64], k[:, 64:128]
    w_in_r = np.ascontiguousarray(np.concatenate([q, k0, k0, k1, k1, v, zu, zv], axis=1))

    def pk(vec):
        return np.asarray(vec, f).reshape(8, 128).T

    g_mrg = np.concatenate([np.asarray(inp["g_attn_out"][0], f), np.asarray(inp["g_sgu_out"][0], f)])
    gvec = np.ascontiguousarray(np.concatenate([pk(inp["g_mix"][0]), pk(inp["g_ffn"][0]), pk(inp["g_ple"][0]), pk(g_mrg)], axis=1))
    cw = np.concatenate([np.asarray(inp["conv_w"][0], f), np.asarray(inp["conv_b"][0], f)[None]], axis=0)
    convp = np.ascontiguousarray(cw.reshape(4, NFC, 128).transpose(2, 1, 0).reshape(128, NFC * 4))
    bs8 = np.ascontiguousarray(np.asarray(inp["sgu_b"][0], f))
    ind16 = np.ascontiguousarray(np.tile(np.kron(np.eye(8, dtype=f), np.ones((1, 64), f)), (2, 1)))
    wsT = np.ascontiguousarray(np.asarray(inp["sgu_w"][0], f).transpose(2, 0, 1).reshape(128, 8 * 128))
    s_ = np.arange(128)[:, None, None]
    j_ = np.arange(3)[None, :, None]
    q_ = np.arange(128)[None, None, :]
    dist = np.abs(q_ + 128 - (j_ * 128 + s_)).astype(f)
    dtab = np.ascontiguousarray(np.where(dist <= 128, dist, f(1e6)).astype(f).reshape(128, 384))
    common = {
        "w_in_r": w_in_r,
        "w_out": np.ascontiguousarray(inp["w_out"][0], dtype=f),
        "w_up": np.ascontiguousarray(inp["w_up"][0], dtype=f),
        "w_down": np.ascontiguousarray(inp["w_down"][0], dtype=f),
        "w_pg": np.ascontiguousarray(inp["w_ple_gate"][0], dtype=f),
        "w_pp": np.ascontiguousarray(inp["w_ple_proj"][0], dtype=f),
        "gvec": gvec, "convp": convp, "bs8": bs8, "ind16": ind16, "wsT": wsT, "dtab": dtab,
        "ident": np.eye(128, dtype=f),
        "sink": np.ascontiguousarray(inp["attn_sink"][0], dtype=f),
        "lng": np.ascontiguousarray(inp["sgu_ln_g"][0], dtype=f),
        "lnb": np.ascontiguousarray(inp["sgu_ln_b"][0], dtype=f),
        "gfin": np.ascontiguousarray(inp["g_final"], dtype=f),
    }
    return common


def kernel(**inputs):
    debug = bool(inputs.pop("_debug", False))
    x = np.asarray(inputs["x"], np.float32)
    p = np.asarray(inputs["p"], np.float32)[0]
    common = _host_layout(inputs)
    key = ("nc", debug)
    if key not in _NC_CACHE:
        _NC_CACHE[key] = build_nc(debug)
    nc = _NC_CACHE[key]
    in_maps = []
    for c in range(8):
        m = dict(common)
        m["x"] = np.ascontiguousarray(x[2 * c:2 * c + 2])
        m["p"] = np.ascontiguousarray(p[2 * c:2 * c + 2])
        in_maps.append(m)
    res = run_bass_kernel_spmd(nc, in_maps, core_ids=list(range(8)))
    out = np.concatenate([r["out"] for r in res.results], axis=0).astype(np.float32)
    if debug:
        kernel.dbg = [np.concatenate([r[n] for r in res.results], axis=0) for n in ("dbg1", "dbg2")]
    return out
```
